# Optimizing a Trainium2 kernel written in Bass

```python
import jax, jax.numpy as jnp
from jax import lax
import numpy as np

D_MODEL = 1024
BATCH = 32
SEQ = 2048
DEPTH = 4

N_A_LAYERS = DEPTH // 2
N_B_LAYERS = DEPTH - N_A_LAYERS

M_EXPAND = 2
M_D_INNER = M_EXPAND * D_MODEL
M_HEADDIM = 64
M_HEADS = M_D_INNER // M_HEADDIM
M_GROUPS = 4
M_D_STATE = 128
M_D_CONV = 4
M_CHUNK = 128
M_CONV_DIM = M_D_INNER + 2 * M_GROUPS * M_D_STATE
M_IN_DIM = 2 * M_D_INNER + 2 * M_GROUPS * M_D_STATE + M_HEADS

SB_HEAD_DIM = 64
SB_HEADS = D_MODEL // SB_HEAD_DIM
SB_WIDTH = SB_HEADS * SB_HEAD_DIM
SB_BLOCK = 128

EPS = 1e-6

kernel_name = 'yoco_mamba2_stickbreaking_adaln'


def _rmsnorm(x, g):
    xf = x.astype(jnp.float32)
    xf = xf * lax.rsqrt(jnp.mean(xf * xf, axis=-1, keepdims=True) + EPS)
    return (xf * g.astype(jnp.float32)).astype(x.dtype)


def _modulate(x, g, shift, scale):
    return _rmsnorm(x, g) * (1.0 + scale[:, None, :]) + shift[:, None, :]


def _causal_dwconv(u, w, b):
    k = w.shape[0]
    out = lax.conv_general_dilated(
        u, w[:, None, :].astype(u.dtype), window_strides=(1,), padding=[(k - 1, 0)],
        dimension_numbers=('NWC', 'WIO', 'NWC'), feature_group_count=u.shape[-1])
    return out + b


def _ssd_chunked(xs, dt, a_neg, bm, cm):
    b, s, h, p = xs.shape
    g, n = bm.shape[2], bm.shape[3]
    hg = h // g
    nc = s // M_CHUNK
    xdt = (xs.astype(jnp.float32) * dt[..., None]).reshape(b, s, g, hg, p)
    a = (dt * a_neg).reshape(b, s, g, hg)

    def chunks(t):
        return jnp.moveaxis(t.reshape((b, nc, M_CHUNK) + t.shape[2:]), 1, 0)

    causal = jnp.tril(jnp.ones((M_CHUNK, M_CHUNK), dtype=bool))[None, :, :, None, None]

    def step(state, inp):
        xc, ac, bc, cc = inp
        acum = jnp.cumsum(ac, axis=1)
        seg = jnp.where(causal, acum[:, :, None] - acum[:, None, :], -jnp.inf)
        scores = jnp.einsum('btgn,bsgn->btsg', cc, bc)[..., None] * jnp.exp(seg)
        y_diag = jnp.einsum('btsgh,bsghp->btghp', scores, xc)
        y_off = jnp.einsum('btgn,bghpn->btghp', cc, state) * jnp.exp(acum)[..., None]
        w_end = jnp.exp(acum[:, -1:] - acum)
        new_state = (state * jnp.exp(acum[:, -1])[..., None, None]
                     + jnp.einsum('bsgn,bsghp->bghpn', bc, xc * w_end[..., None]))
        return new_state, y_diag + y_off

    state0 = jnp.zeros((b, g, hg, p, n), jnp.float32)
    _, ys = lax.scan(step, state0, (chunks(xdt), chunks(a),
                                    chunks(bm.astype(jnp.float32)), chunks(cm.astype(jnp.float32))))
    return jnp.moveaxis(ys, 0, 1).reshape(b, s, h, p)


def _mamba2_mixer(h, in_w, conv_w, conv_b, dt_bias, a_log, d_skip, norm_g, out_w):
    b, s, _ = h.shape
    zxbcdt = h @ in_w
    z = zxbcdt[..., :M_D_INNER]
    xbc = zxbcdt[..., M_D_INNER:M_D_INNER + M_CONV_DIM]
    dt = zxbcdt[..., M_D_INNER + M_CONV_DIM:]
    xbc = jax.nn.silu(_causal_dwconv(xbc, conv_w, conv_b))
    gn = M_GROUPS * M_D_STATE
    xs = xbc[..., :M_D_INNER].reshape(b, s, M_HEADS, M_HEADDIM)
    bm = xbc[..., M_D_INNER:M_D_INNER + gn].reshape(b, s, M_GROUPS, M_D_STATE)
    cm = xbc[..., M_D_INNER + gn:].reshape(b, s, M_GROUPS, M_D_STATE)
    dt = jax.nn.softplus(dt.astype(jnp.float32) + dt_bias.astype(jnp.float32))
    a_neg = -jnp.exp(a_log.astype(jnp.float32))
    y = _ssd_chunked(xs, dt, a_neg, bm, cm)
    y = y + d_skip.astype(jnp.float32)[:, None] * xs.astype(jnp.float32)
    y = y.reshape(b, s, M_D_INNER) * jax.nn.silu(z.astype(jnp.float32))
    yg = y.reshape(b, s, M_GROUPS, M_D_INNER // M_GROUPS)
    yg = yg * lax.rsqrt(jnp.mean(yg * yg, axis=-1, keepdims=True) + EPS)
    y = yg.reshape(b, s, M_D_INNER) * norm_g.astype(jnp.float32)
    return y.astype(h.dtype) @ out_w


def _stick_breaking(q, k, v):
    s_len = q.shape[1]
    scale = SB_HEAD_DIM ** -0.5
    outs = []
    for i in range(s_len // SB_BLOCK):
        lo, hi = i * SB_BLOCK, (i + 1) * SB_BLOCK
        z = jnp.einsum('bthd,bshd->bhts', q[:, lo:hi], k[:, :hi]).astype(jnp.float32) * scale
        mask = ((lo + jnp.arange(SB_BLOCK))[:, None] > jnp.arange(hi)[None, :])[None, None]
        log_keep = jnp.where(mask, jax.nn.log_sigmoid(-z), 0.0)
        after = lax.cumsum(log_keep, axis=3, reverse=True) - log_keep
        log_a = jnp.where(mask, jax.nn.log_sigmoid(z) + after, -jnp.inf)
        a = jnp.exp(log_a).astype(v.dtype)
        outs.append(jnp.einsum('bhts,bshd->bthd', a, v[:, :hi]))
    return jnp.concatenate(outs, axis=1)


def _sb_mixer(h, k, v, in_w, out_w):
    b, s, _ = h.shape
    qz = h @ in_w
    q = qz[..., :SB_WIDTH].reshape(b, s, SB_HEADS, SB_HEAD_DIM)
    z = qz[..., SB_WIDTH:]
    o = _stick_breaking(q, k, v).reshape(b, s, SB_WIDTH)
    return (o * jax.nn.silu(z)) @ out_w


def setup_inputs(seed: int = 0) -> dict:
    key = jax.random.key(seed)
    ks = jax.random.split(key, 24)
    f32 = jnp.float32

    def nrm(k, shape, fan_in, mult=1.0):
        return jax.random.normal(k, shape, f32) * (mult * fan_in ** -0.5)

    def gain(k, shape):
        return 1.0 + 0.05 * jax.random.normal(k, shape, f32)

    def small(k, shape):
        return 0.01 * jax.random.normal(k, shape, f32)

    x = jax.random.normal(ks[0], (BATCH, SEQ, D_MODEL), f32)
    c = jax.random.normal(ks[1], (BATCH, D_MODEL), f32)
    ada_w = nrm(ks[2], (DEPTH, D_MODEL, 3 * D_MODEL), D_MODEL, 0.5)
    ada_b = small(ks[3], (DEPTH, 3 * D_MODEL))
    norm_g = gain(ks[4], (DEPTH, D_MODEL))
    m_in_w = nrm(ks[5], (N_A_LAYERS, D_MODEL, M_IN_DIM), D_MODEL)
    m_conv_w = nrm(ks[6], (N_A_LAYERS, M_D_CONV, M_CONV_DIM), M_D_CONV)
    m_conv_b = small(ks[7], (N_A_LAYERS, M_CONV_DIM))
    dt0 = jnp.exp(jax.random.uniform(ks[8], (N_A_LAYERS, M_HEADS), f32,
                                     np.log(1e-3).astype(np.float32), np.log(1e-1).astype(np.float32)))
    m_dt_bias = dt0 + jnp.log(-jnp.expm1(-dt0))
    m_a_log = jnp.log(jax.random.uniform(ks[9], (N_A_LAYERS, M_HEADS), f32, 1.0, 16.0))
    m_d = 1.0 + 0.1 * jax.random.normal(ks[10], (N_A_LAYERS, M_HEADS), f32)
    m_norm_g = gain(ks[11], (N_A_LAYERS, M_D_INNER))
    m_out_w = nrm(ks[12], (N_A_LAYERS, M_D_INNER, D_MODEL), M_D_INNER)
    kv_ada_w = nrm(ks[13], (D_MODEL, 2 * D_MODEL), D_MODEL, 0.5)
    kv_ada_b = small(ks[14], (2 * D_MODEL,))
    kv_norm_g = gain(ks[15], (D_MODEL,))
    kv_w = nrm(ks[16], (D_MODEL, 2 * SB_WIDTH), D_MODEL)
    sb_in_w = nrm(ks[17], (N_B_LAYERS, D_MODEL, 2 * SB_WIDTH), D_MODEL)
    sb_out_w = nrm(ks[18], (N_B_LAYERS, SB_WIDTH, D_MODEL), SB_WIDTH)
    final_g = gain(ks[19], (D_MODEL,))
    return {'x': x, 'c': c, 'ada_w': ada_w, 'ada_b': ada_b, 'norm_g': norm_g,
            'm_in_w': m_in_w, 'm_conv_w': m_conv_w, 'm_conv_b': m_conv_b,
            'm_dt_bias': m_dt_bias, 'm_a_log': m_a_log, 'm_d': m_d,
            'm_norm_g': m_norm_g, 'm_out_w': m_out_w,
            'kv_ada_w': kv_ada_w, 'kv_ada_b': kv_ada_b, 'kv_norm_g': kv_norm_g, 'kv_w': kv_w,
            'sb_in_w': sb_in_w, 'sb_out_w': sb_out_w, 'final_g': final_g}


def reference(x, c, ada_w, ada_b, norm_g, m_in_w, m_conv_w, m_conv_b, m_dt_bias, m_a_log,
              m_d, m_norm_g, m_out_w, kv_ada_w, kv_ada_b, kv_norm_g, kv_w,
              sb_in_w, sb_out_w, final_g):
    b, s, _ = x.shape
    c_act = jax.nn.silu(c)

    for i in range(N_A_LAYERS):
        shift, scale, gate = jnp.split(c_act @ ada_w[i] + ada_b[i], 3, axis=-1)
        h = _modulate(x, norm_g[i], shift, scale)
        y = _mamba2_mixer(h, m_in_w[i], m_conv_w[i], m_conv_b[i], m_dt_bias[i], m_a_log[i],
                          m_d[i], m_norm_g[i], m_out_w[i])
        x = x + gate[:, None, :] * y

    kv_shift, kv_scale = jnp.split(c_act @ kv_ada_w + kv_ada_b, 2, axis=-1)
    hk = _modulate(x, kv_norm_g, kv_shift, kv_scale)
    kv = hk @ kv_w
    k = kv[..., :SB_WIDTH].reshape(b, s, SB_HEADS, SB_HEAD_DIM)
    v = kv[..., SB_WIDTH:].reshape(b, s, SB_HEADS, SB_HEAD_DIM)

    for j in range(N_B_LAYERS):
        i = N_A_LAYERS + j
        shift, scale, gate = jnp.split(c_act @ ada_w[i] + ada_b[i], 3, axis=-1)
        h = _modulate(x, norm_g[i], shift, scale)
        y = _sb_mixer(h, k, v, sb_in_w[j], sb_out_w[j])
        x = x + gate[:, None, :] * y

    return _rmsnorm(x, final_g)
```

```python
import contextlib
import os
import numpy as np
import concourse.bass as bass
import concourse.mybir as mybir
from concourse.bass_utils import run_bass_kernel_spmd

F32 = mybir.dt.float32
BF = mybir.dt.bfloat16
AF = mybir.ActivationFunctionType
ALU = mybir.AluOpType

NCORES = 8
S = 2048
D = 1024
DI = 2048
NH = 32
NIN = 5152
EPS = 1e-6
MAXV = 20000
DBG = int(os.environ.get('MK_DBG', '99'))
SUB = int(os.environ.get('MK_SUB', '99'))
S2 = int(os.environ.get('MK_S2', '99'))
REORDER = int(os.environ.get('MK_REORDER', '1'))


class Sem:
    def __init__(self, h, sid, owner):
        self.h = h
        self.id = sid
        self.owner = owner


class Buf:
    def __init__(self, name, excl=False):
        self.name = name
        self.excl = excl
        self.w = None
        self.r = {}
        self.dsem = None
        self.dcnt = 0
        self.dlast = None


class Sched:
    ENG = ('pe', 'act', 'dve', 'pool', 'sp')

    def __init__(self, nc):
        self.nc = nc
        self.q = {e: [] for e in self.ENG}
        self.sem = {}
        self.cnt = {}
        self.pending = {e: False for e in self.ENG}
        self.seen = {e: {} for e in self.ENG}
        self.nsem = 0
        self.dbufs = []
        self.defer = None
        for e in self.ENG:
            self._newsem(e)

    def _mksem(self, owner):
        h = self.nc.alloc_semaphore(name=f"sm{self.nsem}")
        s = Sem(h, self.nsem, owner)
        self.nsem += 1
        return s

    def _newsem(self, e):
        self.sem[e] = self._mksem(e)
        self.cnt[e] = 0

    def begin(self):
        assert self.defer is None
        self.defer = []

    def op(self, eng, fn, reads=(), writes=(), sig=True, dma=None):
        if self.defer is not None:
            self.defer.append((eng, fn, tuple(reads), tuple(writes), sig, dma))
            return None
        return self._op(eng, fn, reads, writes, sig, dma)

    def flush(self):
        ops = self.defer
        self.defer = None
        if not ops:
            return
        if not REORDER:
            for o in ops:
                self._op(*o)
            return
        units = []
        cur = None
        for idx, o in enumerate(ops):
            eng, fn, reads, writes, sig, dma = o
            if eng == 'pe':
                if cur is None:
                    cur = [idx]
                else:
                    cur.append(idx)
                if sig:
                    units.append(cur)
                    cur = None
            else:
                units.append([idx])
        assert cur is None, "open PE group at flush"
        units.sort(key=lambda u: u[0])
        nU = len(units)
        lastw = {}
        readers = {}
        deps = [set() for _ in range(nU)]
        ueng = []
        ucost = []
        udma = []
        for ui, u in enumerate(units):
            eng = ops[u[0]][0]
            ueng.append(eng)
            c = 0.0
            isd = False
            for idx in u:
                _, fn, reads, writes, sig, dma = ops[idx]
                c += getattr(fn, 'cost', 0.3)
                isd = isd or (dma is not None)
                bl = list(reads)
                for b in bl:
                    if id(b) in lastw and lastw[id(b)] != ui:
                        deps[ui].add(lastw[id(b)])
                    if b.excl:
                        for r in readers.get(id(b), ()):
                            if r != ui and ueng[r] != eng:
                                deps[ui].add(r)
                for b in writes:
                    if id(b) in lastw and lastw[id(b)] != ui:
                        deps[ui].add(lastw[id(b)])
                    for r in readers.get(id(b), ()):
                        if r != ui:
                            deps[ui].add(r)
                if dma is not None and ('dmaq', id(dma)) in lastw:
                    deps[ui].add(lastw[('dmaq', id(dma))])
            for idx in u:
                _, fn, reads, writes, sig, dma = ops[idx]
                for b in reads:
                    readers.setdefault(id(b), set()).add(ui)
                for b in writes:
                    lastw[id(b)] = ui
                    readers[id(b)] = set()
                if dma is not None:
                    lastw[('dmaq', id(dma))] = ui
            ucost.append(c)
            udma.append(isd)
        succ = [[] for _ in range(nU)]
        indeg = [0] * nU
        for ui in range(nU):
            for d in deps[ui]:
                succ[d].append(ui)
            indeg[ui] = len(deps[ui])
        cp = [0.0] * nU
        for ui in range(nU - 1, -1, -1):
            m = 0.0
            for v in succ[ui]:
                if cp[v] > m:
                    m = cp[v]
            cp[ui] = ucost[ui] + (2.5 if udma[ui] else 0.0) + m
        efree = {e: 0.0 for e in self.ENG}
        fin = [0.0] * nU
        ready = set(ui for ui in range(nU) if indeg[ui] == 0)
        order = []
        while ready:
            best = None
            bkey = None
            for ui in ready:
                e = ueng[ui]
                est = efree[e]
                for d in deps[ui]:
                    lat = 0.0 if (e == 'pe' and ueng[d] == 'pe') else (0.2 if ueng[d] == e else 0.45)
                    t = fin[d] + lat
                    if t > est:
                        est = t
                key = (est, -cp[ui], ui)
                if bkey is None or key < bkey:
                    bkey = key
                    best = ui
            ui = best
            ready.discard(ui)
            st = bkey[0]
            e = ueng[ui]
            if udma[ui]:
                efree[e] = st + ucost[ui]
                fin[ui] = st + 2.5
            else:
                efree[e] = st + ucost[ui]
                fin[ui] = efree[e]
            order.append(ui)
            for v in succ[ui]:
                indeg[v] -= 1
                if indeg[v] == 0:
                    ready.add(v)
        assert len(order) == nU
        for ui in order:
            for idx in units[ui]:
                self._op(*ops[idx])

    def _op(self, eng, fn, reads=(), writes=(), sig=True, dma=None):
        deps = {}

        def add(ev):
            if ev is None:
                return
            sem, val = ev
            if sem.id not in deps or deps[sem.id][1] < val:
                deps[sem.id] = (sem, val)

        for b in reads:
            add(b.w)
            if b.excl:
                for r in b.r.values():
                    if r[0].owner != eng:
                        add(r)
        for b in writes:
            add(b.w)
            for r in b.r.values():
                add(r)
        if dma is not None:
            if dma.dsem is None:
                dma.dsem = self._mksem('dma')
                self.dbufs.append(dma)
            add(dma.dlast)
        for sid, (sem, val) in deps.items():
            if eng == 'pe' and sem.owner == 'pe':
                continue
            if self.seen[eng].get(sid, 0) >= val:
                continue
            self.q[eng].append(('w', sem, val))
            self.seen[eng][sid] = val
        if dma is not None:
            dma.dcnt += 16
            ev = (dma.dsem, dma.dcnt)
            dma.dlast = ev
            self.q[eng].append(('d', fn, dma.dsem))
        else:
            if self.cnt[eng] >= MAXV and not self.pending[eng]:
                self._newsem(eng)
            if sig:
                self.cnt[eng] += 1
                ev = (self.sem[eng], self.cnt[eng])
                self.q[eng].append(('o', fn, self.sem[eng]))
                self.pending[eng] = False
            else:
                ev = (self.sem[eng], self.cnt[eng] + 1)
                self.q[eng].append(('n', fn, None))
                self.pending[eng] = True
        for b in reads:
            b.r[ev[0].id] = ev
        for b in writes:
            b.w = ev
            b.r = {}
        return ev

    def barrier(self):
        assert self.defer is None
        evs = []
        for e in self.ENG:
            assert not self.pending[e]
            if self.cnt[e] > 0:
                evs.append((self.sem[e], self.cnt[e]))
        for b in self.dbufs:
            if b.dlast is not None:
                evs.append(b.dlast)
        for e in self.ENG:
            for sem, val in evs:
                if sem.owner == e:
                    continue
                if self.seen[e].get(sem.id, 0) >= val:
                    continue
                self.q[e].append(('w', sem, val))
                self.seen[e][sem.id] = val

    def emit(self):
        nc = self.nc
        engmap = {'pe': 'tensor', 'act': 'scalar', 'dve': 'vector', 'pool': 'gpsimd', 'sp': 'sync'}
        with nc.Block() as block:
            for e, bname in engmap.items():
                def body(engine, e=e):
                    for item in self.q[e]:
                        k = item[0]
                        if k == 'w':
                            engine.wait_ge(item[1].h, item[2])
                        elif k == 'o':
                            item[1](engine).then_inc(item[2].h, 1)
                        elif k == 'd':
                            item[1](engine).then_inc(item[2].h, 16)
                        else:
                            item[1](engine)
                getattr(block, bname)(body)


def _fsz(ap):
    try:
        n = 1
        for d in list(ap.shape)[1:]:
            n *= int(d)
        return n
    except Exception:
        return 256


def I(method, *args, **kw):
    f = lambda e: getattr(e, method)(*args, **kw)
    try:
        if method == 'matmul':
            rhs = kw['rhs']
            n = max(64, _fsz(rhs))
            mul = 4.0 if rhs.dtype == F32 else 1.0
            f.cost = mul * n / 1800.0 + 0.02
        elif method == 'transpose':
            f.cost = 0.1
        elif method == 'dma_start':
            f.cost = 0.05
        else:
            o = kw.get('out', args[0] if args else None)
            n = _fsz(o)
            f.cost = 0.09 + n / 900.0
            if method in ('scalar_tensor_tensor',):
                f.cost = 0.09 + n / 650.0
    except Exception:
        f.cost = 0.3
    f.method = method
    return f


class Ctx:
    def __init__(self, nc):
        self.nc = nc
        self.S = Sched(nc)
        self.es = contextlib.ExitStack()
        self.n = 0

    def sb(self, shape, dt, es=None):
        self.n += 1
        return (es or self.es).enter_context(self.nc.sbuf_tensor(f"t{self.n}", list(shape), dt))

    def bufs(self, name, n):
        return [Buf(f"{name}{i}") for i in range(n)]


def build_consts():
    i = np.arange(128)
    ident = (i[:, None] == i[None, :]).astype(np.float32)
    ones = np.ones((128, 128), np.float32)
    uincl = (i[:, None] <= i[None, :]).astype(np.float32)
    lstrict = (i[:, None] > i[None, :]).astype(np.float32)
    tincl = (i[:, None] >= i[None, :]).astype(np.float32)
    tc = ones - tincl
    q = np.arange(512)
    masks = [((m * 128 + i[:, None]) < q[None, :]).astype(np.float32) for m in range(4)]
    uincl4 = np.tile(uincl, (1, 4))
    return np.concatenate([ident, ones, uincl, lstrict, tincl, tc] + masks + [uincl4], axis=1)


C_IDENT, C_ONES, C_UINCL, C_LSTRICT, C_TINCL, C_TC = [k * 128 for k in range(6)]
C_MASK = 6 * 128
C_UINCL4 = C_MASK + 4 * 512
NCST = C_UINCL4 + 512


def build_program(nseq, nlayers=4, final=True):
    nc = bass.Bass("TRN2", target_bir_lowering=False)
    cx = Ctx(nc)
    Sd = cx.S

    def din(name, shape, dt=F32):
        return nc.dram_tensor(name, list(shape), dt, kind="ExternalInput").ap()

    x_in = din("x", [nseq, S, D])
    cT_in = din("cT", [128, 8, nseq])
    ada_w = din("ada_w", [4, D, 3 * D])
    ada_bT = din("ada_bT", [4, 128, 16])
    ada_bg = din("ada_bg", [4, D])
    norm_gT = din("norm_gT", [4, 128, 8])
    m_in_w = din("m_in_w", [2, D, NIN])
    m_conv_wT = din("m_conv_wT", [2, 128, 24, 4])
    m_conv_bT = din("m_conv_bT", [2, 128, 24])
    m_dt_bias = din("m_dt_bias", [2, NH])
    m_a_log = din("m_a_log", [2, NH])
    m_d = din("m_d", [2, NH])
    m_norm_gT = din("m_norm_gT", [2, 128, 16])
    m_out_w = din("m_out_w", [2, DI, D])
    kv_ada_w = din("kv_ada_w", [D, 2 * D])
    kv_ada_bT = din("kv_ada_bT", [128, 16])
    kv_norm_gT = din("kv_norm_gT", [128, 8])
    kv_w = din("kv_w", [D, 2 * D])
    sb_in_w = din("sb_in_w", [2, D, 2 * D])
    sb_out_w = din("sb_out_w", [2, D, D])
    final_g = din("final_g", [1, D])
    cst_in = din("cst", [128, NCST])
    out = nc.dram_tensor("out", [nseq, S, D], F32, kind="ExternalOutput").ap()
    xa = nc.dram_tensor("xa", [nseq, S, D], F32, kind="Internal").ap()
    xb = nc.dram_tensor("xb", [nseq, S, D], F32, kind="Internal").ap()
    ysc = nc.dram_tensor("ysc", [nseq, S, DI], BF, kind="Internal").ap()
    ktsc = nc.dram_tensor("ktsc", [nseq, 8, 128, S], BF, kind="Internal").ap()
    vsc = nc.dram_tensor("vsc", [nseq, S, D], BF, kind="Internal").ap()

    def regs(name):
        return [[Buf(f"{name}{b}_{g}") for g in range(4)] for b in range(nseq)]
    R_xa, R_xb, R_y, R_out = regs("xa"), regs("xb"), regs("y"), regs("out")
    R_kv = [Buf(f"kv{b}") for b in range(nseq)]
    R_xin = regs("xin")

    g = contextlib.ExitStack()
    cstf = cx.sb([128, 256], F32, g)
    cstb = cx.sb([128, NCST], BF, g)
    B_cst = Buf("cst")
    Sd.op('sp', I('dma_start', out=cstf[:, :], in_=cst_in[:, C_ONES:C_ONES + 256]), writes=[B_cst], dma=B_cst)
    B_cstb = Buf("cstb")
    identb = cstb[:, C_IDENT:C_IDENT + 128]
    onesf = cstf[:, 0:128]
    uinclf = cstf[:, 128:256]
    lstrictb = cstb[:, C_LSTRICT:C_LSTRICT + 128]
    tinclb = cstb[:, C_TINCL:C_TINCL + 128]
    tcb = cstb[:, C_TC:C_TC + 128]
    uincl4b = cstb[:, C_UINCL4:C_UINCL4 + 512]

    gsT = cx.sb([128, 5, 8, nseq], F32, g)
    shT = cx.sb([128, 5, 8, nseq], F32, g)
    B_mod = Buf("mod")
    B_gate = Buf("gate")
    cact = cx.sb([128, 8, nseq], F32, g)
    B_cact = Buf("cact")
    psum = [g.enter_context(nc.psum_tensor(f"ps{i}", [128, 512], F32)) for i in range(8)]
    PB = [Buf(f"psb{i}", excl=True) for i in range(8)]
    epsc = cx.sb([128, 1], F32, g)
    B_eps = Buf("eps")
    Sd.op('dve', I('memset', epsc[:, :], EPS), writes=[B_eps])

    Sd.op('sp', I('dma_start', out=cact[:, :, :], in_=cT_in), writes=[B_cact], dma=B_cact)
    Sd.op('act', I('activation', out=cact[:, :, :], in_=cact[:, :, :], func=AF.Silu),
          reads=[B_cact], writes=[B_cact])

    def cond_shift_scale(mi, w_ap, bT_ap, gT_ap):
        with contextlib.ExitStack() as es:
            wst = cx.sb([128, 2, 8, 512], F32, es)
            Bw = cx.bufs("wst", 2)
            bT = cx.sb([128, 16], F32, es)
            gT = cx.sb([128, 8], F32, es)
            Bs = Buf("bTgT")
            Sd.op('sp', I('dma_start', out=bT[:, :], in_=bT_ap), writes=[Bs], dma=Bs)
            Bs2 = Buf("gT")
            Sd.op('sp', I('dma_start', out=gT[:, :], in_=gT_ap), writes=[Bs2], dma=Bs2)
            for pc in range(4):
                sl = pc % 2
                Sd.op('sp', I('dma_start',
                    out=wst[:, sl, :, :],
                    in_=w_ap[:, pc * 512:(pc + 1) * 512].rearrange("(c p) n -> p c n", p=128)),
                    writes=[Bw[sl]], dma=Bw[sl])
                for jj in range(4):
                    j = pc * 4 + jj
                    pb = j % 8
                    for k in range(8):
                        Sd.op('pe', I('matmul',
                            psum[pb][:, 0:nseq], lhsT=wst[:, sl, k, jj * 128:(jj + 1) * 128],
                            rhs=cact[:, k, :], start=(k == 0), stop=(k == 7)),
                            reads=[Bw[sl], B_cact], writes=[PB[pb]], sig=(k == 7))
                    if j < 8:
                        Sd.op('dve', I('tensor_scalar',
                            out=shT[:, mi, j, :], in0=psum[pb][:, 0:nseq], scalar1=bT[:, j:j + 1],
                            scalar2=None, op0=ALU.add),
                            reads=[PB[pb], Bs], writes=[B_mod])
                    else:
                        c = j - 8
                        Sd.op('dve', I('tensor_scalar',
                            out=gsT[:, mi, c, :], in0=psum[pb][:, 0:nseq], scalar1=bT[:, j:j + 1],
                            scalar2=1.0, op0=ALU.add, op1=ALU.add),
                            reads=[PB[pb], Bs], writes=[B_mod])
                        Sd.op('dve', I('tensor_scalar',
                            out=gsT[:, mi, c, :], in0=gsT[:, mi, c, :], scalar1=gT[:, c:c + 1],
                            scalar2=None, op0=ALU.mult),
                            reads=[Bs2, B_mod], writes=[B_mod])
            Sd.barrier()

    def cond_gate(li, gate_bc):
        with contextlib.ExitStack() as es:
            cbc = cx.sb([128, 8, nseq, 128], F32, es)
            B_cbc = Buf("cbc")
            for b in range(nseq):
                Sd.op('dve', I('tensor_copy', out=cbc[:, :, b, :],
                                                          in_=cact[:, :, b:b + 1].to_broadcast([128, 8, 128])),
                      reads=[B_cact], writes=[B_cbc])
            wst = cx.sb([128, 2, 8, 512], F32, es)
            Bw = cx.bufs("wsg", 2)
            bg = cx.sb([128, D], F32, es)
            Bbg = Buf("bg")
            Sd.op('sp', I('dma_start', out=bg[:, :], in_=ada_bg[li:li + 1, :].partition_broadcast(128)),
                  writes=[Bbg], dma=Bbg)
            for pc in range(2):
                Sd.op('sp', I('dma_start',
                    out=wst[:, pc, :, :],
                    in_=ada_w[li, :, 2048 + pc * 512:2048 + (pc + 1) * 512].rearrange("(c p) n -> p c n", p=128)),
                    writes=[Bw[pc]], dma=Bw[pc])
                for b in range(nseq):
                    pb = (pc * nseq + b) % 8
                    for k in range(8):
                        Sd.op('pe', I('matmul',
                            psum[pb][:, :], lhsT=cbc[:, k, b, :], rhs=wst[:, pc, k, :],
                            start=(k == 0), stop=(k == 7)),
                            reads=[Bw[pc], B_cbc], writes=[PB[pb]], sig=(k == 7))
                    Sd.op('dve', I('tensor_tensor',
                        out=gate_bc[:, b, pc * 512:(pc + 1) * 512], in0=psum[pb][:, :],
                        in1=bg[:, pc * 512:(pc + 1) * 512], op=ALU.add),
                        reads=[PB[pb], Bbg], writes=[B_gate])
            Sd.barrier()

    Sd.barrier()
    for li in range(nlayers):
        cond_shift_scale(li, ada_w[li], ada_bT[li], norm_gT[li])
    if nlayers >= 2:
        cond_shift_scale(4, kv_ada_w, kv_ada_bT, kv_norm_gT)


    def cast_load(pieces, stage_fn, Bst, Bw, scales=None, Bsc=None):
        for i, (d_ap, s_ap) in enumerate(pieces):
            sl = i % 2
            st_ap = stage_fn(sl, list(d_ap.shape))
            Sd.op('sp', I('dma_start', out=st_ap, in_=s_ap), writes=[Bst[sl]], dma=Bst[sl])
            if scales is not None:
                Sd.op('act', I('activation', out=d_ap, in_=st_ap, func=AF.Identity, scale=scales[i]),
                      reads=[Bst[sl], Bsc], writes=[Bw])
            elif i % 2 == 0:
                Sd.op('act', I('activation', out=d_ap, in_=st_ap, func=AF.Identity), reads=[Bst[sl]], writes=[Bw])
            else:
                Sd.op('dve', I('tensor_copy', out=d_ap, in_=st_ap), reads=[Bst[sl]], writes=[Bw])

    def flat_stage(t2):
        def fn(sl, shape):
            n = 1
            for d in shape[1:]:
                n *= d
            v = t2[:, sl, 0:n]
            if len(shape) == 3:
                v = v.rearrange("p (a b) -> p a b", b=shape[2])
            return v
        return fn
    def load_norm_transpose(es_bufs, x_src, Rsrc, b, grp, mi, keep_x=None):
        xt, Bxt, xn, Bxn, junk, Bjunk, st, Bst, hT, BhT = es_bufs
        for tt in range(4):
            sl = tt % 2
            tok0 = grp * 512 + tt * 128
            if keep_x is not None:
                xdst = keep_x[0][:, tt, :]
                Bx = keep_x[1][tt]
            else:
                xdst = xt[:, sl, :]
                Bx = Bxt[sl]
            Sd.op('sp', I('dma_start', out=xdst, in_=x_src[b, tok0:tok0 + 128, :]),
                  reads=[Rsrc[b][grp]], writes=[Bx], dma=Bx)
            Sd.op('act', I('activation', out=junk[:, :], in_=xdst, func=AF.Square,
                                                                  accum_out=st[:, sl, 0:1]),
                  reads=[Bx], writes=[Bjunk, Bst[sl]])
            Sd.op('act', I('activation', out=st[:, sl, 1:2], in_=st[:, sl, 0:1], func=AF.Ln,
                                                       scale=1.0 / D, bias=epsc[:, 0:1]),
                  reads=[Bst[sl], B_eps], writes=[Bst[sl]])
            Sd.op('act', I('activation', out=st[:, sl, 2:3], in_=st[:, sl, 1:2], func=AF.Exp,
                                                       scale=-0.5),
                  reads=[Bst[sl]], writes=[Bst[sl]])
            Sd.op('dve', I('tensor_scalar',
                out=xn[:, sl, :], in0=xdst, scalar1=st[:, sl, 2:3], scalar2=None, op0=ALU.mult),
                reads=[Bx, Bst[sl]], writes=[Bxn[sl]])
            for c in range(8):
                pb = c // 2
                dst = psum[pb][:, :].bitcast(BF)[:, (c % 2) * 512 + tt * 128:(c % 2) * 512 + (tt + 1) * 128]
                Sd.op('pe', I('transpose',
                    out=dst, in_=xn[:, sl, c * 128:(c + 1) * 128], identity=identb),
                    reads=[Bxn[sl]], writes=[PB[pb]], sig=(c == 7 or c % 2 == 1))
        for c in range(8):
            pb = c // 2
            src = psum[pb][:, :].bitcast(BF)[:, (c % 2) * 512:(c % 2 + 1) * 512]
            Sd.op('act', I('activation',
                out=hT[:, c, :], in_=src, func=AF.Identity,
                scale=gsT[:, mi, c, b:b + 1], bias=shT[:, mi, c, b:b + 1]),
                reads=[PB[pb]], writes=[BhT])

    def norm_bufs(es):
        xt = cx.sb([128, 2, D], F32, es)
        xn = cx.sb([128, 2, D], BF, es)
        junk = cx.sb([128, D], BF, es)
        st = cx.sb([128, 2, 4], F32, es)
        hT = cx.sb([128, 8, 512], BF, es)
        return (xt, cx.bufs("xt", 2), xn, cx.bufs("xn", 2), junk, Buf("junk"), st, cx.bufs("st", 2), hT, Buf("hT"))


    with contextlib.ExitStack() as es0:
        cst_stage = cx.sb([128, 2, 1024], F32, es0)
        Bcs = cx.bufs("cstage", 2)
        pcs = []
        for c0 in range(0, NCST, 1024):
            c1 = min(NCST, c0 + 1024)
            pcs.append((cstb[:, c0:c1], cst_in[:, c0:c1]))
        cast_load(pcs, flat_stage(cst_stage), Bcs, B_cstb)
        Sd.barrier()

    def final_only(x_src, Rsrc):
        with contextlib.ExitStack() as es:
            xt = cx.sb([128, 2, D], F32, es)
            Bxt = cx.bufs("fxt", 2)
            xo = cx.sb([128, 2, D], F32, es)
            Bxo = cx.bufs("fxo", 2)
            junk = cx.sb([128, D], BF, es)
            Bj = Buf("fj")
            st = cx.sb([128, 2, 4], F32, es)
            Bst = cx.bufs("fst", 2)
            fg = cx.sb([128, D], F32, es)
            Bfg = Buf("fg")
            Sd.op('sp', I('dma_start', out=fg[:, :], in_=final_g.partition_broadcast(128)),
                  writes=[Bfg], dma=Bfg)
            for b in range(nseq):
                for t in range(16):
                    sl = t % 2
                    Sd.op('sp', I('dma_start', out=xt[:, sl, :],
                                                                       in_=x_src[b, t * 128:(t + 1) * 128, :]),
                          reads=[Rsrc[b][t // 4]], writes=[Bxt[sl]], dma=Bxt[sl])
                    final_norm_store(xt[:, sl, :], Bxt[sl], xo[:, sl, :], Bxo[sl], junk, Bj, st, Bst[sl], sl,
                                     fg, Bfg, b, t)
            Sd.barrier()

    def final_norm_store(xsrc, Bx, xo, Bxo, junk, Bj, st, Bst, sl, fg, Bfg, b, t):
        Sd.op('act', I('activation', out=junk[:, :], in_=xsrc, func=AF.Square, accum_out=st[:, sl, 0:1]),
              reads=[Bx], writes=[Bj, Bst])
        Sd.op('act', I('activation', out=st[:, sl, 1:2], in_=st[:, sl, 0:1], func=AF.Ln,
                                            scale=1.0 / D, bias=epsc[:, 0:1]),
              reads=[Bst, B_eps], writes=[Bst])
        Sd.op('act', I('activation', out=st[:, sl, 2:3], in_=st[:, sl, 1:2], func=AF.Exp, scale=-0.5),
              reads=[Bst], writes=[Bst])
        Sd.op('dve', I('scalar_tensor_tensor', out=xo, in0=xsrc, scalar=st[:, sl, 2:3], in1=fg[:, :],
                                                      op0=ALU.mult, op1=ALU.mult),
              reads=[Bx, Bst, Bfg], writes=[Bxo])
        Sd.op('sp', I('dma_start', out=out[b, t * 128:(t + 1) * 128, :], in_=xo),
              reads=[Bxo], writes=[R_out[b][t // 4]], dma=Bxo)


    def mamba_A(li, x_src, Rsrc):
        with contextlib.ExitStack() as es:
            inw = cx.sb([128, 8, NIN], BF, es)
            B_inw = Buf("inw")
            cw = cx.sb([128, 24, 4], F32, es)
            cb = cx.sb([128, 24], F32, es)
            dtb = cx.sb([128, NH], F32, es)
            aneg = cx.sb([128, NH], F32, es)
            dsk = cx.sb([128, NH], F32, es)
            B_par = [Buf(f"mpar{i}") for i in range(5)]
            Sd.op('sp', I('dma_start', out=cw[:, :, :], in_=m_conv_wT[li]), writes=[B_par[0]], dma=B_par[0])
            Sd.op('sp', I('dma_start', out=cb[:, :], in_=m_conv_bT[li]), writes=[B_par[1]], dma=B_par[1])
            Sd.op('sp', I('dma_start', out=dtb[:, :], in_=m_dt_bias[li:li + 1, :].partition_broadcast(128)),
                  writes=[B_par[2]], dma=B_par[2])
            Sd.op('sp', I('dma_start', out=aneg[:, :], in_=m_a_log[li:li + 1, :].partition_broadcast(128)),
                  writes=[B_par[3]], dma=B_par[3])
            Sd.op('sp', I('dma_start', out=dsk[:, :], in_=m_d[li:li + 1, :].partition_broadcast(128)),
                  writes=[B_par[4]], dma=B_par[4])
            Sd.op('act', I('activation', out=aneg[:, :], in_=aneg[:, :], func=AF.Exp),
                  reads=[B_par[3]], writes=[B_par[3]])
            Sd.op('dve', I('tensor_scalar', out=aneg[:, :], in0=aneg[:, :], scalar1=-1.0, scalar2=None,
                                                   op0=ALU.mult), reads=[B_par[3]], writes=[B_par[3]])
            nb = norm_bufs(es)
            hT, BhT = nb[8], nb[9]
            U = cx.sb([128, 2, 515], F32, es); BU = cx.bufs("U", 2)
            acc = cx.sb([128, 2, 512], F32, es); Bacc = cx.bufs("acc", 2)
            halo = cx.sb([128, 24, 3], F32, es); Bhalo = cx.bufs("halo", 24)
            xsr = cx.sb([128, 2, 512], BF, es); Bxsr = cx.bufs("xsr", 2)
            BT = cx.sb([128, 4, 512], BF, es); BBT = cx.bufs("BT", 4)
            CT = cx.sb([128, 4, 512], BF, es); BCT = cx.bufs("CT", 4)
            xs_tok = cx.sb([128, 4, DI], BF, es); Bxs = Buf("xstok")
            Btok = cx.sb([128, 4, 512], BF, es); BBtok = Buf("Btok")
            xs_f32 = xs_tok[:, :, :].bitcast(F32)

            def stA(sl, shape):
                v = xs_f32[:, 2 * sl:2 * sl + 2, :].rearrange("p a b -> p (a b)")
                return v[:, 0:shape[1] * shape[2]].rearrange("p (a b) -> p a b", b=shape[2])
            Bsta = cx.bufs("stA", 2)
            pcs = []
            for c0 in range(0, NIN, 256):
                c1 = min(NIN, c0 + 256)
                pcs.append((inw[:, :, c0:c1], m_in_w[li, :, c0:c1].rearrange("(c p) n -> p c n", p=128)))
            cast_load(pcs, stA, Bsta, B_inw)
            Sd.barrier()
            sz = cx.sb([128, 2, DI], BF, es); Bsz = cx.bufs("sz", 2)
            sm = cx.sb([128, 2, 10, NH], F32, es); Bsm = [cx.bufs(f"sm{q}_", 10) for q in range(2)]
            xdt = cx.sb([128, 2, 512], BF, es); Bxdt = cx.bufs("xdt", 2)
            xsD = cx.sb([128, 2, 512], BF, es); BxsD = cx.bufs("xsD", 2)
            xdtw = cx.sb([128, 2, 512], BF, es); Bxdtw = cx.bufs("xdtw", 2)
            Sm = cx.sb([128, 2, 4, 128], BF, es); BSm = cx.bufs("Sm", 2)
            R4 = cx.sb([128, 2, 512], BF, es); BR4 = cx.bufs("R4", 2)
            Ed = cx.sb([128, 2, 512], BF, es); BEd = cx.bufs("Ed", 2)
            WT = cx.sb([128, 2, 8, 128], BF, es); BWT = cx.bufs("WT", 2)
            tmp = cx.sb([128, 2, 512], F32, es); Btmp = cx.bufs("tmp", 2)
            yg = cx.sb([128, 2, 512], F32, es); Byg = cx.bufs("yg", 2)
            ys = cx.sb([128, 2, 4], F32, es); Bys = cx.bufs("ys", 2)
            ybf = cx.sb([128, 2, DI], BF, es); Bybf = cx.bufs("ybf", 2)
            st32 = cx.sb([128, 4, 512], F32, es); Bst32 = cx.bufs("st32", 4)
            stbf = cx.sb([128, 4, 512], BF, es); Bstbf = cx.bufs("stbf", 4)

            for b in range(nseq):
                Sd.op('dve', I('memset', st32[:, :, :], 0.0), writes=Bst32)
                Sd.op('dve', I('memset', stbf[:, :, :], 0.0), writes=Bstbf)
                Sd.op('dve', I('memset', halo[:, :, :], 0.0), writes=Bhalo)
                for grp in range(4):
                    if DBG < 1:
                        continue
                    Sd.begin()
                    load_norm_transpose(nb, x_src, Rsrc, b, grp, li)
                    def cA(cc):
                        pb = 4 + (cc % 2)
                        col0 = DI + cc * 128
                        for k in range(8):
                            Sd.op('pe', I('matmul', psum[pb][:, :], lhsT=inw[:, k, col0:col0 + 128], rhs=hT[:, k, :],
                                          start=(k == 0), stop=(k == 7)),
                                  reads=[B_inw, BhT], writes=[PB[pb]], sig=(k == 7))

                    def cB(cc):
                        pb = 4 + (cc % 2)
                        sl = cc % 2
                        Sd.op('dve', I('tensor_copy', out=U[:, sl, 0:3], in_=halo[:, cc, :]),
                              reads=[Bhalo[cc]], writes=[BU[sl]])
                        Sd.op('act', I('activation', out=U[:, sl, 3:515], in_=psum[pb][:, :], func=AF.Identity),
                              reads=[PB[pb]], writes=[BU[sl]])
                        Sd.op('dve', I('tensor_copy', out=halo[:, cc, :], in_=U[:, sl, 512:515]),
                              reads=[BU[sl]], writes=[Bhalo[cc]])

                    def cC(cc):
                        sl = cc % 2
                        Sd.op('dve', I('tensor_scalar', out=acc[:, sl, :], in0=U[:, sl, 0:512], scalar1=cw[:, cc, 0:1],
                                       scalar2=cb[:, cc:cc + 1], op0=ALU.mult, op1=ALU.add),
                              reads=[BU[sl], B_par[0], B_par[1]], writes=[Bacc[sl]])
                        for tap in range(1, 4):
                            Sd.op('dve', I('scalar_tensor_tensor', out=acc[:, sl, :], in0=U[:, sl, tap:tap + 512],
                                           scalar=cw[:, cc, tap:tap + 1], in1=acc[:, sl, :], op0=ALU.mult, op1=ALU.add),
                                  reads=[BU[sl], B_par[0], Bacc[sl]], writes=[Bacc[sl]])

                    def cdst(cc):
                        sl = cc % 2
                        if cc < 16:
                            return xsr[:, sl, :], Bxsr[sl]
                        elif cc < 20:
                            return BT[:, cc - 16, :], BBT[cc - 16]
                        return CT[:, cc - 20, :], BCT[cc - 20]

                    def cD(cc):
                        sl = cc % 2
                        dst, Bd = cdst(cc)
                        Sd.op('act', I('activation', out=dst, in_=acc[:, sl, :], func=AF.Silu),
                              reads=[Bacc[sl]], writes=[Bd])

                    def cE(cc):
                        if cc >= 20:
                            return
                        dst, Bd = cdst(cc)
                        tb = 2 + (cc % 2)
                        for tt in range(4):
                            Sd.op('pe', I('transpose', out=psum[tb][:, :].bitcast(BF)[:, tt * 128:(tt + 1) * 128],
                                          in_=dst[:, tt * 128:(tt + 1) * 128], identity=identb),
                                  reads=[Bd], writes=[PB[tb]], sig=(tt == 3))

                    def cF(cc):
                        if cc >= 20:
                            return
                        tb = 2 + (cc % 2)
                        src = psum[tb][:, :].bitcast(BF)[:, 0:512].rearrange("p (t c) -> p t c", c=128)
                        if cc < 16:
                            Sd.op('dve', I('tensor_copy', out=xs_tok[:, :, cc * 128:(cc + 1) * 128], in_=src),
                                  reads=[PB[tb]], writes=[Bxs])
                        else:
                            gq = cc - 16
                            Sd.op('dve', I('tensor_copy', out=Btok[:, :, gq * 128:(gq + 1) * 128], in_=src),
                                  reads=[PB[tb]], writes=[BBtok])

                    NCC = 24 if DBG >= 2 else 0
                    for it in range(-2, NCC + 1):
                        if 0 <= it + 2 < NCC:
                            cA(it + 2)
                        if 0 <= it - 1 < NCC:
                            cE(it - 1)
                        if 0 <= it + 1 < NCC:
                            cB(it + 1)
                        if 0 <= it < NCC:
                            cD(it)
                        if 0 <= it + 1 < NCC:
                            cC(it + 1)
                        if 0 <= it - 1 < NCC:
                            cF(it - 1)
                    def prologue(tt):
                        ps_ = tt % 2
                        tsl = slice(tt * 128, (tt + 1) * 128)
                        smv = lambda k: sm[:, ps_, k, :]
                        Bs_ = Bsm[ps_]
                        for zc in range(4):
                            pb = zc % 2
                            for k in range(8):
                                Sd.op('pe', I('matmul', psum[pb][:, :], lhsT=hT[:, k, tsl],
                                              rhs=inw[:, k, zc * 512:(zc + 1) * 512], start=(k == 0), stop=(k == 7)),
                                      reads=[B_inw, BhT], writes=[PB[pb]], sig=(k == 7))
                            Sd.op('act', I('activation', out=sz[:, ps_, zc * 512:(zc + 1) * 512], in_=psum[pb][:, :],
                                           func=AF.Silu),
                                  reads=[PB[pb]], writes=[Bsz[ps_]])
                        for k in range(8):
                            Sd.op('pe', I('matmul', psum[2][:, 0:NH], lhsT=hT[:, k, tsl], rhs=inw[:, k, 5120:5152],
                                          start=(k == 0), stop=(k == 7)),
                                  reads=[B_inw, BhT], writes=[PB[2]], sig=(k == 7))
                        Sd.op('dve', I('tensor_tensor', out=smv(0), in0=psum[2][:, 0:NH], in1=dtb[:, :], op=ALU.add),
                              reads=[PB[2], B_par[2]], writes=[Bs_[0]])
                        Sd.op('act', I('activation', out=smv(1), in_=smv(0), func=AF.Exp),
                              reads=[Bs_[0]], writes=[Bs_[1]])
                        Sd.op('act', I('activation', out=smv(2), in_=smv(1), func=AF.Ln, bias=1.0),
                              reads=[Bs_[1]], writes=[Bs_[2]])
                        Sd.op('dve', I('tensor_tensor', out=smv(3), in0=smv(2), in1=aneg[:, :], op=ALU.mult),
                              reads=[Bs_[2], B_par[3]], writes=[Bs_[3]])
                        Sd.op('pe', I('matmul', psum[2][:, 64:64 + NH], lhsT=uinclf, rhs=smv(3), start=True, stop=True),
                              reads=[Bs_[3]], writes=[PB[2]], sig=False)
                        Sd.op('pe', I('matmul', psum[2][:, 128:128 + NH], lhsT=onesf, rhs=smv(3), start=True, stop=True),
                              reads=[Bs_[3]], writes=[PB[2]], sig=True)
                        acum_ps = psum[2][:, 64:64 + NH]
                        alast_ps = psum[2][:, 128:128 + NH]
                        Sd.op('dve', I('tensor_copy', out=smv(4), in_=acum_ps), reads=[PB[2]], writes=[Bs_[4]])
                        Sd.op('act', I('activation', out=smv(5), in_=acum_ps, func=AF.Exp), reads=[PB[2]], writes=[Bs_[5]])
                        Sd.op('dve', I('tensor_tensor', out=smv(6), in0=alast_ps, in1=smv(4), op=ALU.subtract),
                              reads=[PB[2], Bs_[4]], writes=[Bs_[6]])
                        Sd.op('act', I('activation', out=smv(7), in_=smv(6), func=AF.Exp), reads=[Bs_[6]], writes=[Bs_[7]])
                        Sd.op('act', I('activation', out=smv(8), in_=alast_ps, func=AF.Exp), reads=[PB[2]], writes=[Bs_[8]])
                        Sd.op('dve', I('tensor_tensor', out=smv(9), in0=smv(2), in1=smv(7), op=ALU.mult),
                              reads=[Bs_[2], Bs_[7]], writes=[Bs_[9]])
                        for gq in range(4):
                            Sd.op('pe', I('matmul', psum[3][:, gq * 128:(gq + 1) * 128], lhsT=BT[:, gq, tsl],
                                          rhs=CT[:, gq, tsl], start=True, stop=True),
                                  reads=[BBT[gq], BCT[gq]], writes=[PB[3]], sig=(gq == 3))
                        Sd.op('dve', I('tensor_tensor', out=Sm[:, ps_, :, :],
                                       in0=psum[3][:, :].rearrange("p (g t) -> p g t", t=128),
                                       in1=uincl4b.rearrange("p (g t) -> p g t", t=128), op=ALU.mult),
                              reads=[PB[3], B_cstb], writes=[BSm[ps_]])

                    def front(tt, gq):
                        ps_ = tt % 2
                        smv = lambda k, sl_: sm[:, ps_, k, sl_]
                        Bs_ = Bsm[ps_]
                        gs = gq % 2
                        gsl = slice(gq * 512, (gq + 1) * 512)
                        h8 = slice(gq * 8, (gq + 1) * 8)
                        xv = xs_tok[:, tt, gsl].rearrange("p (h d) -> p h d", d=64)
                        for (dstt, Bdst, slot) in ((xdt, Bxdt, 2), (xsD, BxsD, None), (xdtw, Bxdtw, 9)):
                            if slot is None:
                                sc, Bsc = dsk[:, h8], B_par[4]
                            else:
                                sc, Bsc = smv(slot, h8), Bs_[slot]
                            Sd.op('dve', I('tensor_tensor', out=dstt[:, gs, :].rearrange("p (h d) -> p h d", d=64),
                                           in0=xv, in1=sc.unsqueeze(2).to_broadcast([128, 8, 64]), op=ALU.mult),
                                  reads=[Bxs, Bsc], writes=[Bdst[gs]])
                        for hh in range(2):
                            hq = gq * 2 + hh
                            rs = hq % 2
                            for h4 in range(4):
                                hd = hq * 4 + h4
                                Sd.op('act', I('activation', out=R4[:, rs, h4 * 128:(h4 + 1) * 128],
                                               in_=uincl4b[:, 0:128], func=AF.Identity, scale=smv(3, slice(hd, hd + 1))),
                                      reads=[B_cstb, Bs_[3]], writes=[BR4[rs]])
                            db = 4 + (hq % 2)
                            Sd.op('pe', I('matmul', psum[db][:, :], lhsT=lstrictb, rhs=R4[:, rs, :], start=True, stop=True),
                                  reads=[BR4[rs], B_cstb], writes=[PB[db]])
                            Sd.op('act', I('activation', out=Ed[:, rs, :], in_=psum[db][:, :], func=AF.Exp),
                                  reads=[PB[db]], writes=[BEd[rs]])
                            Sd.op('dve', I('tensor_tensor', out=WT[:, gs, hh * 4:(hh + 1) * 4, :],
                                           in0=Ed[:, rs, :].rearrange("p (h t) -> p h t", t=128),
                                           in1=Sm[:, ps_, gq:gq + 1, :].to_broadcast([128, 4, 128]), op=ALU.mult),
                                  reads=[BEd[rs], BSm[ps_]], writes=[BWT[gs]])

                    def back(tt, gq):
                        ps_ = tt % 2
                        tsl = slice(tt * 128, (tt + 1) * 128)
                        smv = lambda k, sl_: sm[:, ps_, k, sl_]
                        Bs_ = Bsm[ps_]
                        gs = gq % 2
                        gsl = slice(gq * 512, (gq + 1) * 512)
                        h8 = slice(gq * 8, (gq + 1) * 8)
                        for h in range(8):
                            Sd.op('pe', I('matmul', psum[6][:, h * 64:(h + 1) * 64], lhsT=identb,
                                          rhs=xsD[:, gs, h * 64:(h + 1) * 64], start=True, stop=False),
                                  reads=[BxsD[gs], B_cstb], writes=[PB[6]], sig=False)
                            Sd.op('pe', I('matmul', psum[6][:, h * 64:(h + 1) * 64], lhsT=WT[:, gs, h, :],
                                          rhs=xdt[:, gs, h * 64:(h + 1) * 64], start=False, stop=True),
                                  reads=[BWT[gs], Bxdt[gs]], writes=[PB[6]], sig=(h == 7))
                        Sd.op('pe', I('matmul', psum[7][:, :], lhsT=CT[:, gq, tsl], rhs=stbf[:, gq, :], start=True, stop=True),
                              reads=[BCT[gq], Bstbf[gq]], writes=[PB[7]])
                        Sd.op('dve', I('tensor_tensor', out=tmp[:, gs, :].rearrange("p (h d) -> p h d", d=64),
                                       in0=psum[7][:, :].rearrange("p (h d) -> p h d", d=64),
                                       in1=smv(5, h8).unsqueeze(2).to_broadcast([128, 8, 64]), op=ALU.mult),
                              reads=[PB[7], Bs_[5]], writes=[Btmp[gs]])
                        Sd.op('dve', I('tensor_tensor', out=yg[:, gs, :], in0=psum[6][:, :], in1=tmp[:, gs, :], op=ALU.add),
                              reads=[PB[6], Btmp[gs]], writes=[Byg[gs]])
                        Sd.op('dve', I('tensor_tensor', out=yg[:, gs, :], in0=yg[:, gs, :], in1=sz[:, ps_, gsl], op=ALU.mult),
                              reads=[Byg[gs], Bsz[ps_]], writes=[Byg[gs]])
                        Sd.op('act', I('activation', out=tmp[:, gs, :], in_=yg[:, gs, :], func=AF.Square,
                                       accum_out=ys[:, gs, 0:1]),
                              reads=[Byg[gs]], writes=[Btmp[gs], Bys[gs]])
                        Sd.op('act', I('activation', out=ys[:, gs, 1:2], in_=ys[:, gs, 0:1], func=AF.Ln,
                                       scale=1.0 / 512, bias=epsc[:, 0:1]),
                              reads=[Bys[gs], B_eps], writes=[Bys[gs]])
                        Sd.op('act', I('activation', out=ys[:, gs, 2:3], in_=ys[:, gs, 1:2], func=AF.Exp, scale=-0.5),
                              reads=[Bys[gs]], writes=[Bys[gs]])
                        Sd.op('dve', I('tensor_scalar', out=ybf[:, ps_, gsl], in0=yg[:, gs, :], scalar1=ys[:, gs, 2:3],
                                       scalar2=None, op0=ALU.mult),
                              reads=[Byg[gs], Bys[gs]], writes=[Bybf[ps_]])
                        Sd.op('pe', I('matmul', psum[1][:, :], lhsT=Btok[:, tt, gq * 128:(gq + 1) * 128], rhs=xdtw[:, gs, :],
                                      start=True, stop=True),
                              reads=[BBtok, Bxdtw[gs]], writes=[PB[1]])
                        Sd.op('dve', I('tensor_tensor', out=st32[:, gq, :].rearrange("p (h d) -> p h d", d=64),
                                       in0=st32[:, gq, :].rearrange("p (h d) -> p h d", d=64),
                                       in1=smv(8, h8).unsqueeze(2).to_broadcast([128, 8, 64]), op=ALU.mult),
                              reads=[Bs_[8]], writes=[Bst32[gq]])
                        Sd.op('dve', I('tensor_tensor', out=st32[:, gq, :], in0=psum[1][:, :], in1=st32[:, gq, :], op=ALU.add),
                              reads=[PB[1]], writes=[Bst32[gq]])
                        Sd.op('act', I('activation', out=stbf[:, gq, :], in_=st32[:, gq, :], func=AF.Identity),
                              reads=[Bst32[gq]], writes=[Bstbf[gq]])

                    if DBG >= 3:
                        seq_ = [(tt, gq) for tt in range(4) for gq in range(4)]
                        prologue(0)
                        front(0, 0)
                        for idx, (tt, gq) in enumerate(seq_):
                            if gq == 1 and tt < 3:
                                prologue(tt + 1)
                            if idx + 1 < len(seq_):
                                front(*seq_[idx + 1])
                            back(tt, gq)
                            if gq == 3:
                                tok0 = grp * 512 + tt * 128
                                Sd.op('sp', I('dma_start', out=ysc[b, tok0:tok0 + 128, :], in_=ybf[:, tt % 2, :]),
                                      reads=[Bybf[tt % 2]], writes=[R_y[b][grp]], dma=Bybf[tt % 2])
                    Sd.flush()
            Sd.barrier()

    def mamba_B(li, x_src, Rsrc, x_dst, Rdst, do_kv):
        with contextlib.ExitStack() as es:
            gate_bc = cx.sb([128, nseq, D], F32, es)
            cond_gate(li, gate_bc)
            ow = cx.sb([128, 16, D], BF, es)
            B_ow = Buf("ow")
            ngT = cx.sb([128, 16], F32, es)
            B_ng = Buf("ngT")
            Sd.op('sp', I('dma_start', out=ngT[:, :], in_=m_norm_gT[li]), writes=[B_ng], dma=B_ng)
            if do_kv:
                kvw = cx.sb([128, 8, 2 * D], BF, es)
                B_kvw = Buf("kvw")
                xn = cx.sb([128, D], BF, es); Bxn = Buf("bxn")
                junk = cx.sb([128, D], BF, es); Bjunk = Buf("bjunk")
                st = cx.sb([128, 4], F32, es); Bst = Buf("bst")
                hk = cx.sb([128, 8, 512], BF, es); Bhk = Buf("hk")
                kts = cx.sb([128, 8, 512], BF, es); Bkts = Buf("kts")
                vs = cx.sb([128, 2, D], BF, es); Bvs = cx.bufs("vs", 2)
            yt = cx.sb([128, 2, DI], BF, es); Byt = cx.bufs("yt", 2)
            yT = cx.sb([128, 2, 16, 128], BF, es); ByT = cx.bufs("yT", 2)
            xt = cx.sb([128, 2, D], F32, es); Bxt = cx.bufs("bxt", 2)
            xo = cx.sb([128, 2, D], F32, es); Bxo = cx.bufs("bxo", 2)
            cast_load([(ow[:, c, :], m_out_w[li, c * 128:(c + 1) * 128, :]) for c in range(16)],
                      flat_stage(xo), Bxo, B_ow, scales=[ngT[:, c:c + 1] for c in range(16)], Bsc=B_ng)
            if do_kv:
                pcs = []
                for c in range(8):
                    for hf in range(2):
                        pcs.append((kvw[:, c, hf * 1024:(hf + 1) * 1024],
                                    kv_w[c * 128:(c + 1) * 128, hf * 1024:(hf + 1) * 1024]))
                cast_load(pcs, flat_stage(xo), Bxo, B_kvw)
            for b in range(nseq):
                for grp in range(4):
                    Sd.begin()
                    for tt in range(4):
                        t = grp * 4 + tt
                        sl = t % 2
                        tok0 = t * 128
                        Sd.op('sp', I('dma_start', out=yt[:, sl, :],
                                                                            in_=ysc[b, tok0:tok0 + 128, :]),
                              reads=[R_y[b][grp]], writes=[Byt[sl]], dma=Byt[sl])
                        Sd.op('sp', I('dma_start', out=xt[:, sl, :],
                                                                            in_=x_src[b, tok0:tok0 + 128, :]),
                              reads=[Rsrc[b][grp]], writes=[Bxt[sl]], dma=Bxt[sl])
                        for c in range(16):
                            pb = c // 4
                            Sd.op('pe', I('transpose',
                                out=psum[pb][:, :].bitcast(BF)[:, (c % 4) * 128:(c % 4 + 1) * 128],
                                in_=yt[:, sl, c * 128:(c + 1) * 128], identity=identb),
                                reads=[Byt[sl]], writes=[PB[pb]], sig=(c % 4 == 3))
                        for pb in range(4):
                            src = psum[pb][:, :].bitcast(BF)[:, 0:512].rearrange("p (c t) -> p c t", t=128)
                            if pb % 2 == 0:
                                Sd.op('act', I('activation', out=yT[:, sl, pb * 4:(pb + 1) * 4, :], in_=src,
                                               func=AF.Identity),
                                      reads=[PB[pb]], writes=[ByT[sl]])
                            else:
                                Sd.op('dve', I('tensor_copy', out=yT[:, sl, pb * 4:(pb + 1) * 4, :], in_=src),
                                      reads=[PB[pb]], writes=[ByT[sl]])
                        for hf in range(2):
                            pb = 4 + hf
                            for c in range(16):
                                Sd.op('pe', I('matmul',
                                    psum[pb][:, :], lhsT=yT[:, sl, c, :], rhs=ow[:, c, hf * 512:(hf + 1) * 512],
                                    start=(c == 0), stop=(c == 15)),
                                    reads=[ByT[sl], B_ow], writes=[PB[pb]], sig=(c == 15))
                            Sd.op('dve', I('tensor_tensor',
                                out=xo[:, sl, hf * 512:(hf + 1) * 512], in0=psum[pb][:, :],
                                in1=gate_bc[:, b, hf * 512:(hf + 1) * 512], op=ALU.mult),
                                reads=[PB[pb], B_gate], writes=[Bxo[sl]])
                        Sd.op('dve', I('tensor_tensor', out=xo[:, sl, :], in0=xo[:, sl, :],
                                                                      in1=xt[:, sl, :], op=ALU.add),
                              reads=[Bxt[sl]], writes=[Bxo[sl]])
                        Sd.op('sp', I('dma_start', out=x_dst[b, tok0:tok0 + 128, :],
                                                                            in_=xo[:, sl, :]),
                              reads=[Bxo[sl]], writes=[Rdst[b][grp]], dma=Bxo[sl])
                        if do_kv:
                            Sd.op('act', I('activation', out=junk[:, :], in_=xo[:, sl, :], func=AF.Square,
                                                                       accum_out=st[:, 0:1]),
                                  reads=[Bxo[sl]], writes=[Bjunk, Bst])
                            Sd.op('act', I('activation', out=st[:, 1:2], in_=st[:, 0:1], func=AF.Ln,
                                                                scale=1.0 / D, bias=epsc[:, 0:1]),
                                  reads=[Bst, B_eps], writes=[Bst])
                            Sd.op('act', I('activation', out=st[:, 2:3], in_=st[:, 1:2], func=AF.Exp, scale=-0.5),
                                  reads=[Bst], writes=[Bst])
                            Sd.op('dve', I('tensor_scalar', out=xn[:, :], in0=xo[:, sl, :],
                                                                          scalar1=st[:, 2:3], scalar2=None, op0=ALU.mult),
                                  reads=[Bxo[sl], Bst], writes=[Bxn])
                            for c in range(8):
                                pb = 6 + c // 4
                                Sd.op('pe', I('transpose',
                                    out=psum[pb][:, :].bitcast(BF)[:, (c % 4) * 128:(c % 4 + 1) * 128],
                                    in_=xn[:, c * 128:(c + 1) * 128], identity=identb),
                                    reads=[Bxn], writes=[PB[pb]], sig=(c % 4 == 3))
                            for c in range(8):
                                pb = 6 + c // 4
                                Sd.op('act', I('activation',
                                    out=hk[:, c, tt * 128:(tt + 1) * 128],
                                    in_=psum[pb][:, :].bitcast(BF)[:, (c % 4) * 128:(c % 4 + 1) * 128],
                                    func=AF.Identity, scale=gsT[:, 4, c, b:b + 1], bias=shT[:, 4, c, b:b + 1]),
                                    reads=[PB[pb], B_mod], writes=[Bhk])
                            for hf in range(2):
                                pb = 4 + hf
                                for k in range(8):
                                    Sd.op('pe', I('matmul',
                                        psum[pb][:, :], lhsT=hk[:, k, tt * 128:(tt + 1) * 128],
                                        rhs=kvw[:, k, D + hf * 512:D + (hf + 1) * 512], start=(k == 0), stop=(k == 7)),
                                        reads=[Bhk, B_kvw], writes=[PB[pb]], sig=(k == 7))
                                Sd.op('act', I('activation',
                                    out=vs[:, sl, hf * 512:(hf + 1) * 512], in_=psum[pb][:, :], func=AF.Identity),
                                    reads=[PB[pb]], writes=[Bvs[sl]])
                            Sd.op('sp', I('dma_start', out=vsc[b, tok0:tok0 + 128, :],
                                                                                in_=vs[:, sl, :]),
                                  reads=[Bvs[sl]], writes=[R_kv[b]], dma=Bvs[sl])
                    if do_kv:
                        for hp in range(8):
                            pb = 6 + hp % 2
                            for k in range(8):
                                Sd.op('pe', I('matmul',
                                    psum[pb][:, :], lhsT=kvw[:, k, hp * 128:(hp + 1) * 128], rhs=hk[:, k, :],
                                    start=(k == 0), stop=(k == 7)),
                                    reads=[Bhk, B_kvw], writes=[PB[pb]], sig=(k == 7))
                            Sd.op('dve', I('tensor_copy', out=kts[:, hp, :], in_=psum[pb][:, :]),
                                  reads=[PB[pb]], writes=[Bkts])
                        Sd.op('sp', I('dma_start',
                            out=ktsc[b, :, :, grp * 512:(grp + 1) * 512].rearrange("c p s -> p c s"),
                            in_=kts[:, :, :]),
                            reads=[Bkts], writes=[R_kv[b]], dma=Bkts)
                    Sd.flush()
            Sd.barrier()


    gsc = nc.dram_tensor("gsc", [4, nseq, D], F32, kind="Internal").ap()
    R_gsc = [Buf(f"gsc{i}") for i in range(4)]

    def gate_to_dram(li):
        with contextlib.ExitStack() as es:
            gate_bc = cx.sb([128, nseq, D], F32, es)
            cond_gate(li, gate_bc)
            Sd.op('sp', I('dma_start', out=gsc[li:li + 1, :, :], in_=gate_bc[0:1, :, :]),
                  reads=[B_gate], writes=[R_gsc[li]], dma=B_gate)
            Sd.barrier()

    def sb_layer(j_l, li, x_src, Rsrc, x_dst, Rdst, last):
        gate_to_dram(li)
        with contextlib.ExitStack() as es:
            inw = cx.sb([128, 8, 2 * D], BF, es); B_inw = Buf("sinw")
            ow = cx.sb([128, 8, D], BF, es); B_ow = Buf("sow")
            KT = cx.sb([128, 8, S], BF, es); B_KT = Buf("KT")
            Vt = cx.sb([128, 16, D], BF, es); B_Vt = Buf("Vt")
            nb = norm_bufs(es)
            xt, Bxt = nb[0], nb[1]
            hT, BhT = nb[8], nb[9]
            GT, BGT = hT, BhT
            QT = cx.sb([128, 8, 512], BF, es); BQT = Buf("QT")
            SZ = cx.sb([128, 8, 512], BF, es); BSZ = Buf("SZ")
            Et = [cx.sb([128, 2, 512], F32, es) for _ in range(2)]
            BE = [cx.bufs(f"E{X}", 2) for X in range(2)]
            SPt = [cx.sb([128, 3, 512], BF, es) for _ in range(2)]
            BSP = [cx.bufs(f"SP{X}", 3) for X in range(2)]
            ECt = [cx.sb([128, 2, 512], BF, es) for _ in range(2)]
            BEC = [cx.bufs(f"EC{X}", 2) for X in range(2)]
            At = [cx.sb([128, 2, 512], BF, es) for _ in range(2)]
            BA = [cx.bufs(f"A{X}", 2) for X in range(2)]
            gate1 = cx.sb([128, D], F32, es); Bg1 = Buf("gate1")
            xo = cx.sb([128, 2, D], F32, es); Bxo = cx.bufs("sxo", 2)
            pcs = []
            for c in range(8):
                for hf in range(2):
                    pcs.append((inw[:, c, hf * 1024:(hf + 1) * 1024],
                                sb_in_w[j_l, c * 128:(c + 1) * 128, hf * 1024:(hf + 1) * 1024]))
            cast_load(pcs, flat_stage(xo), Bxo, B_inw)
            cast_load([(ow[:, c, :], sb_out_w[j_l, c * 128:(c + 1) * 128, :]) for c in range(8)],
                      flat_stage(xo), Bxo, B_ow)
            if last:
                xo2 = cx.sb([128, 2, D], F32, es); Bxo2 = cx.bufs("sxo2", 2)
                fg = cx.sb([128, D], F32, es); Bfg = Buf("sfg")
                Sd.op('sp', I('dma_start', out=fg[:, :], in_=final_g.partition_broadcast(128)),
                      writes=[Bfg], dma=Bfg)
                fjunk, Bfj = nb[4], nb[5]
                fst = cx.sb([128, 2, 4], F32, es); Bfst = cx.bufs("fst2", 2)
            zbank = [[0, 1], [2, 3]]
            pbank = [4, 5]
            OB = 6
            for b in range(nseq):
                for g_ in range(4):
                    Sd.begin()
                    load_norm_transpose(nb, x_src, Rsrc, b, g_, li)
                    for hp in range(8):
                        for k in range(8):
                            Sd.op('pe', I('matmul',
                                psum[7][:, :], lhsT=inw[:, k, hp * 128:(hp + 1) * 128], rhs=hT[:, k, :],
                                start=(k == 0), stop=(k == 7)),
                                reads=[B_inw, BhT], writes=[PB[7]], sig=(k == 7))
                        Sd.op('act', I('activation', out=QT[:, hp, :], in_=psum[7][:, :],
                                                                   func=AF.Identity, scale=0.125),
                              reads=[PB[7]], writes=[BQT])
                        for k in range(8):
                            Sd.op('pe', I('matmul',
                                psum[6][:, :], lhsT=inw[:, k, D + hp * 128:D + (hp + 1) * 128], rhs=hT[:, k, :],
                                start=(k == 0), stop=(k == 7)),
                                reads=[B_inw, BhT], writes=[PB[6]], sig=(k == 7))
                        Sd.op('act', I('activation', out=SZ[:, hp, :], in_=psum[6][:, :], func=AF.Silu),
                              reads=[PB[6]], writes=[BSZ])
                    Sd.flush()
                    if g_ == 0:
                        Sd.op('sp', I('dma_start', out=KT[:, :, :], in_=ktsc[b].rearrange("c p s -> p c s")),
                              reads=[R_kv[b]], writes=[B_KT], dma=B_KT)
                        Sd.op('sp', I('dma_start', out=Vt[:, :, :], in_=vsc[b].rearrange("(k p) d -> p k d", p=128)),
                              reads=[R_kv[b]], writes=[B_Vt], dma=B_Vt)
                        Sd.op('sp', I('dma_start', out=gate1[:, :], in_=gsc[li, b:b + 1, :].partition_broadcast(128)),
                              reads=[R_gsc[li]], writes=[Bg1], dma=Bg1)
                    J = 4 * g_ + 4
                    tiles = [(hp, k) for hp in range(8) for k in range(J)]
                    NT = len(tiles)

                    def c0_of(i):
                        hp, k = tiles[i]
                        jb = J - 1 - k
                        return max(0, jb - 4 * g_) * 128

                    def s1_qk(X, i):
                        hp, k = tiles[i]
                        jb = J - 1 - k
                        c0 = c0_of(i)
                        pr = slice(X * 64, X * 64 + 64)
                        zb = zbank[X][i % 2]
                        Sd.op('pe', I('matmul', psum[zb][:, c0:512], lhsT=KT[pr, hp, jb * 128:(jb + 1) * 128],
                                      rhs=QT[pr, hp, c0:512], start=True, stop=True),
                              reads=[B_KT, BQT], writes=[PB[zb]])

                    def s1_e(X, i):
                        zb = zbank[X][i % 2]
                        c0 = c0_of(i)
                        Sd.op('act', I('activation', out=Et[X][:, i % 2, c0:512], in_=psum[zb][:, c0:512], func=AF.Exp),
                              reads=[PB[zb]], writes=[BE[X][i % 2]])

                    def s1_mask(X, i):
                        hp, k = tiles[i]
                        jb = J - 1 - k
                        if jb >= 4 * g_:
                            c0 = c0_of(i)
                            Sd.op('dve', I('tensor_tensor', out=Et[X][:, i % 2, c0:c0 + 128],
                                           in0=Et[X][:, i % 2, c0:c0 + 128],
                                           in1=cstb[:, C_MASK:C_MASK + 128], op=ALU.mult),
                                  reads=[B_cstb], writes=[BE[X][i % 2]])

                    def s1_sp(X, i):
                        c0 = c0_of(i)
                        Sd.op('act', I('activation', out=SPt[X][:, i % 3, c0:512], in_=Et[X][:, i % 2, c0:512],
                                       func=AF.Ln, bias=1.0),
                              reads=[BE[X][i % 2]], writes=[BSP[X][i % 3]])

                    def s2_cum(X, i):
                        hp, k = tiles[i]
                        pbk = pbank[X]
                        first = (k == 0)
                        c0 = c0_of(i)
                        if not first:
                            cp_ = c0_of(i - 1)
                            Sd.op('pe', I('matmul', psum[pbk][:, cp_:512], lhsT=tcb, rhs=SPt[X][:, (i - 1) % 3, cp_:512],
                                          start=False, stop=False, skip_group_check=True),
                                  reads=[BSP[X][(i - 1) % 3], B_cstb], writes=[PB[pbk]], sig=False)
                        Sd.op('pe', I('matmul', psum[pbk][:, c0:512], lhsT=tinclb, rhs=SPt[X][:, i % 3, c0:512],
                                      start=first, stop=True, skip_group_check=True),
                              reads=[BSP[X][i % 3], B_cstb], writes=[PB[pbk]])

                    def s2_ec(X, i):
                        pbk = pbank[X]
                        c0 = c0_of(i)
                        Sd.op('act', I('activation', out=ECt[X][:, i % 2, c0:512], in_=psum[pbk][:, c0:512],
                                       func=AF.Exp, scale=-1.0),
                              reads=[PB[pbk]], writes=[BEC[X][i % 2]])

                    def s2_a(X, i):
                        c0 = c0_of(i)
                        Sd.op('dve', I('tensor_tensor', out=At[X][:, i % 2, c0:512], in0=Et[X][:, i % 2, c0:512],
                                       in1=ECt[X][:, i % 2, c0:512], op=ALU.mult),
                              reads=[BE[X][i % 2], BEC[X][i % 2]], writes=[BA[X][i % 2]])

                    def s2_av(X, i):
                        hp, k = tiles[i]
                        jb = J - 1 - k
                        hd = hp * 2 + X
                        ob = 6 + (hp % 2)
                        c0 = c0_of(i)
                        Sd.op('pe', I('matmul', psum[ob][X * 64:X * 64 + 64, c0:512],
                                      lhsT=Vt[:, jb, hd * 64:(hd + 1) * 64], rhs=At[X][:, i % 2, c0:512],
                                      start=(k == 0), stop=(k == J - 1), skip_group_check=True),
                              reads=[B_Vt, BA[X][i % 2]], writes=[PB[ob]])
                        if X == 1 and k == J - 1:
                            Sd.op('dve', I('tensor_tensor', out=GT[:, hp, :], in0=psum[ob][:, :],
                                           in1=SZ[:, hp, :], op=ALU.mult),
                                  reads=[PB[ob], BSZ], writes=[BGT])

                    for it in range(-2, NT):
                        i2, i1, i0 = it + 2, it + 1, it
                        if 0 <= i2 < NT:
                            s1_qk(0, i2); s1_qk(1, i2)
                        if 0 <= i1 < NT:
                            s2_cum(0, i1); s2_cum(1, i1)
                        if 0 <= i0 < NT:
                            s2_av(0, i0); s2_av(1, i0)
                        if 0 <= i2 < NT:
                            s1_e(0, i2); s1_e(1, i2)
                            s1_mask(0, i2); s1_mask(1, i2)
                            s1_sp(0, i2); s1_sp(1, i2)
                        if 0 <= i1 < NT:
                            s2_ec(0, i1); s2_ec(1, i1)
                            s2_a(0, i1); s2_a(1, i1)
                    Sd.begin()
                    for tt in range(4):
                        t = g_ * 4 + tt
                        sl = t % 2
                        tok0 = t * 128
                        Sd.op('sp', I('dma_start', out=xt[:, sl, :],
                                                                            in_=x_src[b, tok0:tok0 + 128, :]),
                              reads=[Rsrc[b][g_]], writes=[Bxt[sl]], dma=Bxt[sl])
                        for hf in range(2):
                            pb = 0 + hf
                            for hp in range(8):
                                Sd.op('pe', I('matmul',
                                    psum[pb][:, :], lhsT=GT[:, hp, tt * 128:(tt + 1) * 128],
                                    rhs=ow[:, hp, hf * 512:(hf + 1) * 512], start=(hp == 0), stop=(hp == 7)),
                                    reads=[BGT, B_ow], writes=[PB[pb]], sig=(hp == 7))
                            Sd.op('dve', I('tensor_tensor',
                                out=xo[:, sl, hf * 512:(hf + 1) * 512], in0=psum[pb][:, :],
                                in1=gate1[:, hf * 512:(hf + 1) * 512], op=ALU.mult),
                                reads=[PB[pb], Bg1], writes=[Bxo[sl]])
                        Sd.op('dve', I('tensor_tensor', out=xo[:, sl, :], in0=xo[:, sl, :],
                                                                      in1=xt[:, sl, :], op=ALU.add),
                              reads=[Bxt[sl]], writes=[Bxo[sl]])
                        if last:
                            final_norm_store(xo[:, sl, :], Bxo[sl], xo2[:, sl, :], Bxo2[sl], fjunk, Bfj, fst, Bfst[sl],
                                             sl, fg, Bfg, b, t)
                        else:
                            Sd.op('sp', I('dma_start', out=x_dst[b, tok0:tok0 + 128, :],
                                                                                in_=xo[:, sl, :]),
                                  reads=[Bxo[sl]], writes=[Rdst[b][g_]], dma=Bxo[sl])
                    Sd.flush()
            Sd.barrier()

    cur_x, cur_R = x_in, R_xin
    if nlayers >= 1:
        mamba_A(0, x_in, R_xin)
        if DBG >= 6:
            mamba_B(0, x_in, R_xin, xa, R_xa, False)
            cur_x, cur_R = xa, R_xa
    if nlayers >= 2:
        mamba_A(1, xa, R_xa)
        mamba_B(1, xa, R_xa, xb, R_xb, True)
        cur_x, cur_R = xb, R_xb
    if nlayers >= 3:
        sb_layer(0, 2, xb, R_xb, xa, R_xa, last=(nlayers == 3))
    if nlayers >= 4:
        sb_layer(1, 3, xa, R_xa, xb, R_xb, last=True)
    if nlayers <= 2:
        final_only(cur_x, cur_R)

    Sd.barrier()
    Sd.emit()
    g.close()
    cx.es.close()
    return nc


_CACHE = {}


def prep_inputs(inputs, nseq, core):
    f = lambda a: np.ascontiguousarray(a, dtype=np.float32)
    b0 = core * nseq
    c = inputs['c'][b0:b0 + nseq]
    cT = f(c.T.reshape(8, 128, nseq).transpose(1, 0, 2))
    ada_b = inputs['ada_b']
    m = {
        'x': f(inputs['x'][b0:b0 + nseq]),
        'cT': cT,
        'ada_w': f(inputs['ada_w']),
        'ada_bT': f(ada_b[:, :2048].reshape(4, 16, 128).transpose(0, 2, 1)),
        'ada_bg': f(ada_b[:, 2048:]),
        'norm_gT': f(inputs['norm_g'].reshape(4, 8, 128).transpose(0, 2, 1)),
        'm_in_w': f(inputs['m_in_w']),
        'm_conv_wT': f(inputs['m_conv_w'].transpose(0, 2, 1).reshape(2, 24, 128, 4).transpose(0, 2, 1, 3)),
        'm_conv_bT': f(inputs['m_conv_b'].reshape(2, 24, 128).transpose(0, 2, 1)),
        'm_dt_bias': f(inputs['m_dt_bias']),
        'm_a_log': f(inputs['m_a_log']),
        'm_d': f(inputs['m_d']),
        'm_norm_gT': f(inputs['m_norm_g'].reshape(2, 16, 128).transpose(0, 2, 1)),
        'm_out_w': f(inputs['m_out_w']),
        'kv_ada_w': f(inputs['kv_ada_w']),
        'kv_ada_bT': f(inputs['kv_ada_b'].reshape(16, 128).T),
        'kv_norm_gT': f(inputs['kv_norm_g'].reshape(8, 128).T),
        'kv_w': f(inputs['kv_w']),
        'sb_in_w': f(inputs['sb_in_w']),
        'sb_out_w': f(inputs['sb_out_w']),
        'final_g': f(inputs['final_g'].reshape(1, D)),
        'cst': build_consts(),
    }
    return m


def run(inputs, nseq, ncores, nlayers=4, final=True, trace=False):
    key = (nseq, nlayers, final)
    if key not in _CACHE:
        _CACHE[key] = build_program(nseq, nlayers, final)
    nc = _CACHE[key]
    inputs = {k: np.asarray(v) for k, v in inputs.items()}
    in_maps = [prep_inputs(inputs, nseq, c) for c in range(ncores)]
    res = run_bass_kernel_spmd(nc, in_maps, core_ids=list(range(ncores)), trace=trace)
    outs = [np.asarray(r["out"]) for r in res.results]
    return np.concatenate(outs, axis=0).astype(np.float32), res


def kernel(**inputs):
    out, _ = run(inputs, 4, NCORES)
    return out
```

```python
import contextlib
import os
import numpy as np
import concourse.bass as bass
import concourse.mybir as mybir
from concourse.bass_utils import run_bass_kernel_spmd

F32 = mybir.dt.float32
BF = mybir.dt.bfloat16
AF = mybir.ActivationFunctionType
ALU = mybir.AluOpType

NCORES = 8
S = 2048
D = 1024
DI = 2048
NH = 32
NIN = 5152
EPS = 1e-6
MAXV = 20000
DBG = int(os.environ.get('MK_DBG', '99'))
SUB = int(os.environ.get('MK_SUB', '99'))
S2 = int(os.environ.get('MK_S2', '99'))
REORDER = int(os.environ.get('MK_REORDER', '1'))


class Sem:
    def __init__(self, h, sid, owner):
        self.h = h
        self.id = sid
        self.owner = owner


class Buf:
    def __init__(self, name, excl=False):
        self.name = name
        self.excl = excl
        self.w = None
        self.r = {}
        self.dsem = None
        self.dcnt = 0
        self.dlast = None


class Sched:
    ENG = ('pe', 'act', 'dve', 'pool', 'sp')

    def __init__(self, nc):
        self.nc = nc
        self.q = {e: [] for e in self.ENG}
        self.sem = {}
        self.cnt = {}
        self.pending = {e: False for e in self.ENG}
        self.seen = {e: {} for e in self.ENG}
        self.nsem = 0
        self.dbufs = []
        self.defer = None
        for e in self.ENG:
            self._newsem(e)

    def _mksem(self, owner):
        h = self.nc.alloc_semaphore(name=f"sm{self.nsem}")
        s = Sem(h, self.nsem, owner)
        self.nsem += 1
        return s

    def _newsem(self, e):
        self.sem[e] = self._mksem(e)
        self.cnt[e] = 0

    def begin(self):
        assert self.defer is None
        self.defer = []

    def op(self, eng, fn, reads=(), writes=(), sig=True, dma=None):
        if self.defer is not None:
            self.defer.append((eng, fn, tuple(reads), tuple(writes), sig, dma))
            return None
        return self._op(eng, fn, reads, writes, sig, dma)

    def flush(self):
        ops = self.defer
        self.defer = None
        if not ops:
            return
        if not REORDER:
            for o in ops:
                self._op(*o)
            return
        units = []
        cur = None
        for idx, o in enumerate(ops):
            eng, fn, reads, writes, sig, dma = o
            if eng == 'pe':
                if cur is None:
                    cur = [idx]
                else:
                    cur.append(idx)
                if sig:
                    units.append(cur)
                    cur = None
            else:
                units.append([idx])
        assert cur is None, "open PE group at flush"
        units.sort(key=lambda u: u[0])
        nU = len(units)
        lastw = {}
        readers = {}
        deps = [set() for _ in range(nU)]
        ueng = []
        ucost = []
        udma = []
        for ui, u in enumerate(units):
            eng = ops[u[0]][0]
            ueng.append(eng)
            c = 0.0
            isd = False
            for idx in u:
                _, fn, reads, writes, sig, dma = ops[idx]
                c += getattr(fn, 'cost', 0.3)
                isd = isd or (dma is not None)
                bl = list(reads)
                for b in bl:
                    if id(b) in lastw and lastw[id(b)] != ui:
                        deps[ui].add(lastw[id(b)])
                    if b.excl:
                        for r in readers.get(id(b), ()):
                            if r != ui and ueng[r] != eng:
                                deps[ui].add(r)
                for b in writes:
                    if id(b) in lastw and lastw[id(b)] != ui:
                        deps[ui].add(lastw[id(b)])
                    for r in readers.get(id(b), ()):
                        if r != ui:
                            deps[ui].add(r)
                if dma is not None and ('dmaq', id(dma)) in lastw:
                    deps[ui].add(lastw[('dmaq', id(dma))])
            for idx in u:
                _, fn, reads, writes, sig, dma = ops[idx]
                for b in reads:
                    readers.setdefault(id(b), set()).add(ui)
                for b in writes:
                    lastw[id(b)] = ui
                    readers[id(b)] = set()
                if dma is not None:
                    lastw[('dmaq', id(dma))] = ui
            ucost.append(c)
            udma.append(isd)
        succ = [[] for _ in range(nU)]
        indeg = [0] * nU
        for ui in range(nU):
            for d in deps[ui]:
                succ[d].append(ui)
            indeg[ui] = len(deps[ui])
        cp = [0.0] * nU
        for ui in range(nU - 1, -1, -1):
            m = 0.0
            for v in succ[ui]:
                if cp[v] > m:
                    m = cp[v]
            cp[ui] = ucost[ui] + (2.5 if udma[ui] else 0.0) + m
        efree = {e: 0.0 for e in self.ENG}
        fin = [0.0] * nU
        ready = set(ui for ui in range(nU) if indeg[ui] == 0)
        order = []
        while ready:
            best = None
            bkey = None
            for ui in ready:
                e = ueng[ui]
                est = efree[e]
                for d in deps[ui]:
                    lat = 0.0 if (e == 'pe' and ueng[d] == 'pe') else (0.2 if ueng[d] == e else 0.45)
                    t = fin[d] + lat
                    if t > est:
                        est = t
                key = (est, -cp[ui], ui)
                if bkey is None or key < bkey:
                    bkey = key
                    best = ui
            ui = best
            ready.discard(ui)
            st = bkey[0]
            e = ueng[ui]
            if udma[ui]:
                efree[e] = st + ucost[ui]
                fin[ui] = st + 2.5
            else:
                efree[e] = st + ucost[ui]
                fin[ui] = efree[e]
            order.append(ui)
            for v in succ[ui]:
                indeg[v] -= 1
                if indeg[v] == 0:
                    ready.add(v)
        assert len(order) == nU
        for ui in order:
            for idx in units[ui]:
                self._op(*ops[idx])

    def _op(self, eng, fn, reads=(), writes=(), sig=True, dma=None):
        deps = {}

        def add(ev):
            if ev is None:
                return
            sem, val = ev
            if sem.id not in deps or deps[sem.id][1] < val:
                deps[sem.id] = (sem, val)

        for b in reads:
            add(b.w)
            if b.excl:
                for r in b.r.values():
                    if r[0].owner != eng:
                        add(r)
        for b in writes:
            add(b.w)
            for r in b.r.values():
                add(r)
        if dma is not None:
            if dma.dsem is None:
                dma.dsem = self._mksem('dma')
                self.dbufs.append(dma)
            add(dma.dlast)
        for sid, (sem, val) in deps.items():
            if eng == 'pe' and sem.owner == 'pe':
                continue
            if self.seen[eng].get(sid, 0) >= val:
                continue
            self.q[eng].append(('w', sem, val))
            self.seen[eng][sid] = val
        if dma is not None:
            dma.dcnt += 16
            ev = (dma.dsem, dma.dcnt)
            dma.dlast = ev
            self.q[eng].append(('d', fn, dma.dsem))
        else:
            if self.cnt[eng] >= MAXV and not self.pending[eng]:
                self._newsem(eng)
            if sig:
                self.cnt[eng] += 1
                ev = (self.sem[eng], self.cnt[eng])
                self.q[eng].append(('o', fn, self.sem[eng]))
                self.pending[eng] = False
            else:
                ev = (self.sem[eng], self.cnt[eng] + 1)
                self.q[eng].append(('n', fn, None))
                self.pending[eng] = True
        for b in reads:
            b.r[ev[0].id] = ev
        for b in writes:
            b.w = ev
            b.r = {}
        return ev

    def barrier(self):
        assert self.defer is None
        evs = []
        for e in self.ENG:
            assert not self.pending[e]
            if self.cnt[e] > 0:
                evs.append((self.sem[e], self.cnt[e]))
        for b in self.dbufs:
            if b.dlast is not None:
                evs.append(b.dlast)
        for e in self.ENG:
            for sem, val in evs:
                if sem.owner == e:
                    continue
                if self.seen[e].get(sem.id, 0) >= val:
                    continue
                self.q[e].append(('w', sem, val))
                self.seen[e][sem.id] = val

    def emit(self):
        nc = self.nc
        engmap = {'pe': 'tensor', 'act': 'scalar', 'dve': 'vector', 'pool': 'gpsimd', 'sp': 'sync'}
        with nc.Block() as block:
            for e, bname in engmap.items():
                def body(engine, e=e):
                    for item in self.q[e]:
                        k = item[0]
                        if k == 'w':
                            engine.wait_ge(item[1].h, item[2])
                        elif k == 'o':
                            item[1](engine).then_inc(item[2].h, 1)
                        elif k == 'd':
                            item[1](engine).then_inc(item[2].h, 16)
                        else:
                            item[1](engine)
                getattr(block, bname)(body)


def _fsz(ap):
    try:
        n = 1
        for d in list(ap.shape)[1:]:
            n *= int(d)
        return n
    except Exception:
        return 256


def I(method, *args, **kw):
    f = lambda e: getattr(e, method)(*args, **kw)
    try:
        if method == 'matmul':
            rhs = kw['rhs']
            n = max(64, _fsz(rhs))
            mul = 4.0 if rhs.dtype == F32 else 1.0
            f.cost = mul * n / 1800.0 + 0.02
        elif method == 'transpose':
            f.cost = 0.1
        elif method == 'dma_start':
            f.cost = 0.05
        else:
            o = kw.get('out', args[0] if args else None)
            n = _fsz(o)
            f.cost = 0.09 + n / 900.0
            if method in ('scalar_tensor_tensor',):
                f.cost = 0.09 + n / 650.0
    except Exception:
        f.cost = 0.3
    f.method = method
    return f


class Ctx:
    def __init__(self, nc):
        self.nc = nc
        self.S = Sched(nc)
        self.es = contextlib.ExitStack()
        self.n = 0

    def sb(self, shape, dt, es=None):
        self.n += 1
        return (es or self.es).enter_context(self.nc.sbuf_tensor(f"t{self.n}", list(shape), dt))

    def bufs(self, name, n):
        return [Buf(f"{name}{i}") for i in range(n)]


def build_consts():
    i = np.arange(128)
    ident = (i[:, None] == i[None, :]).astype(np.float32)
    ones = np.ones((128, 128), np.float32)
    uincl = (i[:, None] <= i[None, :]).astype(np.float32)
    lstrict = (i[:, None] > i[None, :]).astype(np.float32)
    tincl = (i[:, None] >= i[None, :]).astype(np.float32)
    tc = ones - tincl
    q = np.arange(512)
    masks = [((m * 128 + i[:, None]) < q[None, :]).astype(np.float32) for m in range(4)]
    uincl4 = np.tile(uincl, (1, 4))
    return np.concatenate([ident, ones, uincl, lstrict, tincl, tc] + masks + [uincl4], axis=1)


C_IDENT, C_ONES, C_UINCL, C_LSTRICT, C_TINCL, C_TC = [k * 128 for k in range(6)]
C_MASK = 6 * 128
C_UINCL4 = C_MASK + 4 * 512
NCST = C_UINCL4 + 512


def build_program(nseq, nlayers=4, final=True):
    nc = bass.Bass("TRN2", target_bir_lowering=False)
    cx = Ctx(nc)
    Sd = cx.S

    def din(name, shape, dt=F32):
        return nc.dram_tensor(name, list(shape), dt, kind="ExternalInput").ap()

    x_in = din("x", [nseq, S, D])
    cT_in = din("cT", [128, 8, nseq])
    ada_w = din("ada_w", [4, D, 3 * D])
    ada_bT = din("ada_bT", [4, 128, 16])
    ada_bg = din("ada_bg", [4, D])
    norm_gT = din("norm_gT", [4, 128, 8])
    m_in_w = din("m_in_w", [2, D, NIN])
    m_conv_wT = din("m_conv_wT", [2, 128, 24, 4])
    m_conv_bT = din("m_conv_bT", [2, 128, 24])
    m_dt_bias = din("m_dt_bias", [2, NH])
    m_a_log = din("m_a_log", [2, NH])
    m_d = din("m_d", [2, NH])
    m_norm_gT = din("m_norm_gT", [2, 128, 16])
    m_out_w = din("m_out_w", [2, DI, D])
    kv_ada_w = din("kv_ada_w", [D, 2 * D])
    kv_ada_bT = din("kv_ada_bT", [128, 16])
    kv_norm_gT = din("kv_norm_gT", [128, 8])
    kv_w = din("kv_w", [D, 2 * D])
    sb_in_w = din("sb_in_w", [2, D, 2 * D])
    sb_out_w = din("sb_out_w", [2, D, D])
    final_g = din("final_g", [1, D])
    cst_in = din("cst", [128, NCST])
    out = nc.dram_tensor("out", [nseq, S, D], F32, kind="ExternalOutput").ap()
    xa = nc.dram_tensor("xa", [nseq, S, D], F32, kind="Internal").ap()
    xb = nc.dram_tensor("xb", [nseq, S, D], F32, kind="Internal").ap()
    ysc = nc.dram_tensor("ysc", [nseq, S, DI], BF, kind="Internal").ap()
    ktsc = nc.dram_tensor("ktsc", [nseq, 8, 128, S], BF, kind="Internal").ap()
    vsc = nc.dram_tensor("vsc", [nseq, S, D], BF, kind="Internal").ap()

    def regs(name):
        return [[Buf(f"{name}{b}_{g}") for g in range(4)] for b in range(nseq)]
    R_xa, R_xb, R_y, R_out = regs("xa"), regs("xb"), regs("y"), regs("out")
    R_kv = [Buf(f"kv{b}") for b in range(nseq)]
    R_xin = regs("xin")

    g = contextlib.ExitStack()
    cstf = cx.sb([128, 256], F32, g)
    cstb = cx.sb([128, NCST], BF, g)
    B_cst = Buf("cst")
    Sd.op('sp', I('dma_start', out=cstf[:, :], in_=cst_in[:, C_ONES:C_ONES + 256]), writes=[B_cst], dma=B_cst)
    B_cstb = Buf("cstb")
    identb = cstb[:, C_IDENT:C_IDENT + 128]
    onesf = cstf[:, 0:128]
    uinclf = cstf[:, 128:256]
    lstrictb = cstb[:, C_LSTRICT:C_LSTRICT + 128]
    tinclb = cstb[:, C_TINCL:C_TINCL + 128]
    tcb = cstb[:, C_TC:C_TC + 128]
    uincl4b = cstb[:, C_UINCL4:C_UINCL4 + 512]

    gsT = cx.sb([128, 5, 8, nseq], F32, g)
    shT = cx.sb([128, 5, 8, nseq], F32, g)
    B_mod = Buf("mod")
    B_gate = Buf("gate")
    cact = cx.sb([128, 8, nseq], F32, g)
    B_cact = Buf("cact")
    psum = [g.enter_context(nc.psum_tensor(f"ps{i}", [128, 512], F32)) for i in range(8)]
    PB = [Buf(f"psb{i}", excl=True) for i in range(8)]
    epsc = cx.sb([128, 1], F32, g)
    B_eps = Buf("eps")
    Sd.op('dve', I('memset', epsc[:, :], EPS), writes=[B_eps])

    Sd.op('sp', I('dma_start', out=cact[:, :, :], in_=cT_in), writes=[B_cact], dma=B_cact)
    Sd.op('act', I('activation', out=cact[:, :, :], in_=cact[:, :, :], func=AF.Silu),
          reads=[B_cact], writes=[B_cact])

    def cond_shift_scale(mi, w_ap, bT_ap, gT_ap):
        with contextlib.ExitStack() as es:
            wst = cx.sb([128, 2, 8, 512], F32, es)
            Bw = cx.bufs("wst", 2)
            bT = cx.sb([128, 16], F32, es)
            gT = cx.sb([128, 8], F32, es)
            Bs = Buf("bTgT")
            Sd.op('sp', I('dma_start', out=bT[:, :], in_=bT_ap), writes=[Bs], dma=Bs)
            Bs2 = Buf("gT")
            Sd.op('sp', I('dma_start', out=gT[:, :], in_=gT_ap), writes=[Bs2], dma=Bs2)
            for pc in range(4):
                sl = pc % 2
                Sd.op('sp', I('dma_start',
                    out=wst[:, sl, :, :],
                    in_=w_ap[:, pc * 512:(pc + 1) * 512].rearrange("(c p) n -> p c n", p=128)),
                    writes=[Bw[sl]], dma=Bw[sl])
                for jj in range(4):
                    j = pc * 4 + jj
                    pb = j % 8
                    for k in range(8):
                        Sd.op('pe', I('matmul',
                            psum[pb][:, 0:nseq], lhsT=wst[:, sl, k, jj * 128:(jj + 1) * 128],
                            rhs=cact[:, k, :], start=(k == 0), stop=(k == 7)),
                            reads=[Bw[sl], B_cact], writes=[PB[pb]], sig=(k == 7))
                    if j < 8:
                        Sd.op('dve', I('tensor_scalar',
                            out=shT[:, mi, j, :], in0=psum[pb][:, 0:nseq], scalar1=bT[:, j:j + 1],
                            scalar2=None, op0=ALU.add),
                            reads=[PB[pb], Bs], writes=[B_mod])
                    else:
                        c = j - 8
                        Sd.op('dve', I('tensor_scalar',
                            out=gsT[:, mi, c, :], in0=psum[pb][:, 0:nseq], scalar1=bT[:, j:j + 1],
                            scalar2=1.0, op0=ALU.add, op1=ALU.add),
                            reads=[PB[pb], Bs], writes=[B_mod])
                        Sd.op('dve', I('tensor_scalar',
                            out=gsT[:, mi, c, :], in0=gsT[:, mi, c, :], scalar1=gT[:, c:c + 1],
                            scalar2=None, op0=ALU.mult),
                            reads=[Bs2, B_mod], writes=[B_mod])
            Sd.barrier()

    def cond_gate(li, gate_bc):
        with contextlib.ExitStack() as es:
            cbc = cx.sb([128, 8, nseq, 128], F32, es)
            B_cbc = Buf("cbc")
            for b in range(nseq):
                Sd.op('dve', I('tensor_copy', out=cbc[:, :, b, :],
                                                          in_=cact[:, :, b:b + 1].to_broadcast([128, 8, 128])),
                      reads=[B_cact], writes=[B_cbc])
            wst = cx.sb([128, 2, 8, 512], F32, es)
            Bw = cx.bufs("wsg", 2)
            bg = cx.sb([128, D], F32, es)
            Bbg = Buf("bg")
            Sd.op('sp', I('dma_start', out=bg[:, :], in_=ada_bg[li:li + 1, :].partition_broadcast(128)),
                  writes=[Bbg], dma=Bbg)
            for pc in range(2):
                Sd.op('sp', I('dma_start',
                    out=wst[:, pc, :, :],
                    in_=ada_w[li, :, 2048 + pc * 512:2048 + (pc + 1) * 512].rearrange("(c p) n -> p c n", p=128)),
                    writes=[Bw[pc]], dma=Bw[pc])
                for b in range(nseq):
                    pb = (pc * nseq + b) % 8
                    for k in range(8):
                        Sd.op('pe', I('matmul',
                            psum[pb][:, :], lhsT=cbc[:, k, b, :], rhs=wst[:, pc, k, :],
                            start=(k == 0), stop=(k == 7)),
                            reads=[Bw[pc], B_cbc], writes=[PB[pb]], sig=(k == 7))
                    Sd.op('dve', I('tensor_tensor',
                        out=gate_bc[:, b, pc * 512:(pc + 1) * 512], in0=psum[pb][:, :],
                        in1=bg[:, pc * 512:(pc + 1) * 512], op=ALU.add),
                        reads=[PB[pb], Bbg], writes=[B_gate])
            Sd.barrier()

    Sd.barrier()
    for li in range(nlayers):
        cond_shift_scale(li, ada_w[li], ada_bT[li], norm_gT[li])
    if nlayers >= 2:
        cond_shift_scale(4, kv_ada_w, kv_ada_bT, kv_norm_gT)


    def cast_load(pieces, stage_fn, Bst, Bw, scales=None, Bsc=None):
        for i, (d_ap, s_ap) in enumerate(pieces):
            sl = i % 2
            st_ap = stage_fn(sl, list(d_ap.shape))
            Sd.op('sp', I('dma_start', out=st_ap, in_=s_ap), writes=[Bst[sl]], dma=Bst[sl])
            if scales is not None:
                Sd.op('act', I('activation', out=d_ap, in_=st_ap, func=AF.Identity, scale=scales[i]),
                      reads=[Bst[sl], Bsc], writes=[Bw])
            elif i % 2 == 0:
                Sd.op('act', I('activation', out=d_ap, in_=st_ap, func=AF.Identity), reads=[Bst[sl]], writes=[Bw])
            else:
                Sd.op('dve', I('tensor_copy', out=d_ap, in_=st_ap), reads=[Bst[sl]], writes=[Bw])

    def flat_stage(t2):
        def fn(sl, shape):
            n = 1
            for d in shape[1:]:
                n *= d
            v = t2[:, sl, 0:n]
            if len(shape) == 3:
                v = v.rearrange("p (a b) -> p a b", b=shape[2])
            return v
        return fn
    def load_norm_transpose(es_bufs, x_src, Rsrc, b, grp, mi, keep_x=None):
        xt, Bxt, xn, Bxn, junk, Bjunk, st, Bst, hT, BhT = es_bufs
        for tt in range(4):
            sl = tt % 2
            tok0 = grp * 512 + tt * 128
            if keep_x is not None:
                xdst = keep_x[0][:, tt, :]
                Bx = keep_x[1][tt]
            else:
                xdst = xt[:, sl, :]
                Bx = Bxt[sl]
            Sd.op('sp', I('dma_start', out=xdst, in_=x_src[b, tok0:tok0 + 128, :]),
                  reads=[Rsrc[b][grp]], writes=[Bx], dma=Bx)
            Sd.op('act', I('activation', out=junk[:, :], in_=xdst, func=AF.Square,
                                                                  accum_out=st[:, sl, 0:1]),
                  reads=[Bx], writes=[Bjunk, Bst[sl]])
            Sd.op('act', I('activation', out=st[:, sl, 1:2], in_=st[:, sl, 0:1], func=AF.Ln,
                                                       scale=1.0 / D, bias=epsc[:, 0:1]),
                  reads=[Bst[sl], B_eps], writes=[Bst[sl]])
            Sd.op('act', I('activation', out=st[:, sl, 2:3], in_=st[:, sl, 1:2], func=AF.Exp,
                                                       scale=-0.5),
                  reads=[Bst[sl]], writes=[Bst[sl]])
            Sd.op('dve', I('tensor_scalar',
                out=xn[:, sl, :], in0=xdst, scalar1=st[:, sl, 2:3], scalar2=None, op0=ALU.mult),
                reads=[Bx, Bst[sl]], writes=[Bxn[sl]])
            for c in range(8):
                pb = c // 2
                dst = psum[pb][:, :].bitcast(BF)[:, (c % 2) * 512 + tt * 128:(c % 2) * 512 + (tt + 1) * 128]
                Sd.op('pe', I('transpose',
                    out=dst, in_=xn[:, sl, c * 128:(c + 1) * 128], identity=identb),
                    reads=[Bxn[sl]], writes=[PB[pb]], sig=(c == 7 or c % 2 == 1))
        for c in range(8):
            pb = c // 2
            src = psum[pb][:, :].bitcast(BF)[:, (c % 2) * 512:(c % 2 + 1) * 512]
            Sd.op('act', I('activation',
                out=hT[:, c, :], in_=src, func=AF.Identity,
                scale=gsT[:, mi, c, b:b + 1], bias=shT[:, mi, c, b:b + 1]),
                reads=[PB[pb]], writes=[BhT])

    def norm_bufs(es):
        xt = cx.sb([128, 2, D], F32, es)
        xn = cx.sb([128, 2, D], BF, es)
        junk = cx.sb([128, D], BF, es)
        st = cx.sb([128, 2, 4], F32, es)
        hT = cx.sb([128, 8, 512], BF, es)
        return (xt, cx.bufs("xt", 2), xn, cx.bufs("xn", 2), junk, Buf("junk"), st, cx.bufs("st", 2), hT, Buf("hT"))


    with contextlib.ExitStack() as es0:
        cst_stage = cx.sb([128, 2, 1024], F32, es0)
        Bcs = cx.bufs("cstage", 2)
        pcs = []
        for c0 in range(0, NCST, 1024):
            c1 = min(NCST, c0 + 1024)
            pcs.append((cstb[:, c0:c1], cst_in[:, c0:c1]))
        cast_load(pcs, flat_stage(cst_stage), Bcs, B_cstb)
        Sd.barrier()

    def final_only(x_src, Rsrc):
        with contextlib.ExitStack() as es:
            xt = cx.sb([128, 2, D], F32, es)
            Bxt = cx.bufs("fxt", 2)
            xo = cx.sb([128, 2, D], F32, es)
            Bxo = cx.bufs("fxo", 2)
            junk = cx.sb([128, D], BF, es)
            Bj = Buf("fj")
            st = cx.sb([128, 2, 4], F32, es)
            Bst = cx.bufs("fst", 2)
            fg = cx.sb([128, D], F32, es)
            Bfg = Buf("fg")
            Sd.op('sp', I('dma_start', out=fg[:, :], in_=final_g.partition_broadcast(128)),
                  writes=[Bfg], dma=Bfg)
            for b in range(nseq):
                for t in range(16):
                    sl = t % 2
                    Sd.op('sp', I('dma_start', out=xt[:, sl, :],
                                                                       in_=x_src[b, t * 128:(t + 1) * 128, :]),
                          reads=[Rsrc[b][t // 4]], writes=[Bxt[sl]], dma=Bxt[sl])
                    final_norm_store(xt[:, sl, :], Bxt[sl], xo[:, sl, :], Bxo[sl], junk, Bj, st, Bst[sl], sl,
                                     fg, Bfg, b, t)
            Sd.barrier()

    def final_norm_store(xsrc, Bx, xo, Bxo, junk, Bj, st, Bst, sl, fg, Bfg, b, t):
        Sd.op('act', I('activation', out=junk[:, :], in_=xsrc, func=AF.Square, accum_out=st[:, sl, 0:1]),
              reads=[Bx], writes=[Bj, Bst])
        Sd.op('act', I('activation', out=st[:, sl, 1:2], in_=st[:, sl, 0:1], func=AF.Ln,
                                            scale=1.0 / D, bias=epsc[:, 0:1]),
              reads=[Bst, B_eps], writes=[Bst])
        Sd.op('act', I('activation', out=st[:, sl, 2:3], in_=st[:, sl, 1:2], func=AF.Exp, scale=-0.5),
              reads=[Bst], writes=[Bst])
        Sd.op('dve', I('scalar_tensor_tensor', out=xo, in0=xsrc, scalar=st[:, sl, 2:3], in1=fg[:, :],
                                                      op0=ALU.mult, op1=ALU.mult),
              reads=[Bx, Bst, Bfg], writes=[Bxo])
        Sd.op('sp', I('dma_start', out=out[b, t * 128:(t + 1) * 128, :], in_=xo),
              reads=[Bxo], writes=[R_out[b][t // 4]], dma=Bxo)


    def mamba_A(li, x_src, Rsrc):
        with contextlib.ExitStack() as es:
            inw = cx.sb([128, 8, NIN], BF, es)
            B_inw = Buf("inw")
            cw = cx.sb([128, 24, 4], F32, es)
            cb = cx.sb([128, 24], F32, es)
            dtb = cx.sb([128, NH], F32, es)
            aneg = cx.sb([128, NH], F32, es)
            dsk = cx.sb([128, NH], F32, es)
            B_par = [Buf(f"mpar{i}") for i in range(5)]
            Sd.op('sp', I('dma_start', out=cw[:, :, :], in_=m_conv_wT[li]), writes=[B_par[0]], dma=B_par[0])
            Sd.op('sp', I('dma_start', out=cb[:, :], in_=m_conv_bT[li]), writes=[B_par[1]], dma=B_par[1])
            Sd.op('sp', I('dma_start', out=dtb[:, :], in_=m_dt_bias[li:li + 1, :].partition_broadcast(128)),
                  writes=[B_par[2]], dma=B_par[2])
            Sd.op('sp', I('dma_start', out=aneg[:, :], in_=m_a_log[li:li + 1, :].partition_broadcast(128)),
                  writes=[B_par[3]], dma=B_par[3])
            Sd.op('sp', I('dma_start', out=dsk[:, :], in_=m_d[li:li + 1, :].partition_broadcast(128)),
                  writes=[B_par[4]], dma=B_par[4])
            Sd.op('act', I('activation', out=aneg[:, :], in_=aneg[:, :], func=AF.Exp),
                  reads=[B_par[3]], writes=[B_par[3]])
            Sd.op('dve', I('tensor_scalar', out=aneg[:, :], in0=aneg[:, :], scalar1=-1.0, scalar2=None,
                                                   op0=ALU.mult), reads=[B_par[3]], writes=[B_par[3]])
            nb = norm_bufs(es)
            hT, BhT = nb[8], nb[9]
            U = cx.sb([128, 2, 515], F32, es); BU = cx.bufs("U", 2)
            acc = cx.sb([128, 2, 512], F32, es); Bacc = cx.bufs("acc", 2)
            halo = cx.sb([128, 24, 3], F32, es); Bhalo = cx.bufs("halo", 24)
            xsr = cx.sb([128, 2, 512], BF, es); Bxsr = cx.bufs("xsr", 2)
            BT = cx.sb([128, 4, 512], BF, es); BBT = cx.bufs("BT", 4)
            CT = cx.sb([128, 4, 512], BF, es); BCT = cx.bufs("CT", 4)
            xs_tok = cx.sb([128, 4, DI], BF, es); Bxs = Buf("xstok")
            Btok = cx.sb([128, 4, 512], BF, es); BBtok = Buf("Btok")
            xs_f32 = xs_tok[:, :, :].bitcast(F32)

            def stA(sl, shape):
                v = xs_f32[:, 2 * sl:2 * sl + 2, :].rearrange("p a b -> p (a b)")
                return v[:, 0:shape[1] * shape[2]].rearrange("p (a b) -> p a b", b=shape[2])
            Bsta = cx.bufs("stA", 2)
            pcs = []
            for c0 in range(0, NIN, 256):
                c1 = min(NIN, c0 + 256)
                pcs.append((inw[:, :, c0:c1], m_in_w[li, :, c0:c1].rearrange("(c p) n -> p c n", p=128)))
            cast_load(pcs, stA, Bsta, B_inw)
            Sd.barrier()
            sz = cx.sb([128, 2, DI], BF, es); Bsz = cx.bufs("sz", 2)
            sm = cx.sb([128, 2, 10, NH], F32, es); Bsm = [cx.bufs(f"sm{q}_", 10) for q in range(2)]
            xdt = cx.sb([128, 2, 512], BF, es); Bxdt = cx.bufs("xdt", 2)
            xsD = cx.sb([128, 2, 512], BF, es); BxsD = cx.bufs("xsD", 2)
            xdtw = cx.sb([128, 2, 512], BF, es); Bxdtw = cx.bufs("xdtw", 2)
            Sm = cx.sb([128, 2, 4, 128], BF, es); BSm = cx.bufs("Sm", 2)
            R4 = cx.sb([128, 2, 512], BF, es); BR4 = cx.bufs("R4", 2)
            Ed = cx.sb([128, 2, 512], BF, es); BEd = cx.bufs("Ed", 2)
            WT = cx.sb([128, 2, 8, 128], BF, es); BWT = cx.bufs("WT", 2)
            tmp = cx.sb([128, 2, 512], F32, es); Btmp = cx.bufs("tmp", 2)
            yg = cx.sb([128, 2, 512], F32, es); Byg = cx.bufs("yg", 2)
            ys = cx.sb([128, 2, 4], F32, es); Bys = cx.bufs("ys", 2)
            ybf = cx.sb([128, 2, DI], BF, es); Bybf = cx.bufs("ybf", 2)
            st32 = cx.sb([128, 4, 512], F32, es); Bst32 = cx.bufs("st32", 4)
            stbf = cx.sb([128, 4, 512], BF, es); Bstbf = cx.bufs("stbf", 4)

            for b in range(nseq):
                Sd.op('dve', I('memset', st32[:, :, :], 0.0), writes=Bst32)
                Sd.op('dve', I('memset', stbf[:, :, :], 0.0), writes=Bstbf)
                Sd.op('dve', I('memset', halo[:, :, :], 0.0), writes=Bhalo)
                for grp in range(4):
                    if DBG < 1:
                        continue
                    Sd.begin()
                    load_norm_transpose(nb, x_src, Rsrc, b, grp, li)
                    def cA(cc):
                        pb = 4 + (cc % 2)
                        col0 = DI + cc * 128
                        for k in range(8):
                            Sd.op('pe', I('matmul', psum[pb][:, :], lhsT=inw[:, k, col0:col0 + 128], rhs=hT[:, k, :],
                                          start=(k == 0), stop=(k == 7)),
                                  reads=[B_inw, BhT], writes=[PB[pb]], sig=(k == 7))

                    def cB(cc):
                        pb = 4 + (cc % 2)
                        sl = cc % 2
                        Sd.op('dve', I('tensor_copy', out=U[:, sl, 0:3], in_=halo[:, cc, :]),
                              reads=[Bhalo[cc]], writes=[BU[sl]])
                        Sd.op('act', I('activation', out=U[:, sl, 3:515], in_=psum[pb][:, :], func=AF.Identity),
                              reads=[PB[pb]], writes=[BU[sl]])
                        Sd.op('dve', I('tensor_copy', out=halo[:, cc, :], in_=U[:, sl, 512:515]),
                              reads=[BU[sl]], writes=[Bhalo[cc]])

                    def cC(cc):
                        sl = cc % 2
                        Sd.op('dve', I('tensor_scalar', out=acc[:, sl, :], in0=U[:, sl, 0:512], scalar1=cw[:, cc, 0:1],
                                       scalar2=cb[:, cc:cc + 1], op0=ALU.mult, op1=ALU.add),
                              reads=[BU[sl], B_par[0], B_par[1]], writes=[Bacc[sl]])
                        for tap in range(1, 4):
                            Sd.op('dve', I('scalar_tensor_tensor', out=acc[:, sl, :], in0=U[:, sl, tap:tap + 512],
                                           scalar=cw[:, cc, tap:tap + 1], in1=acc[:, sl, :], op0=ALU.mult, op1=ALU.add),
                                  reads=[BU[sl], B_par[0], Bacc[sl]], writes=[Bacc[sl]])

                    def cdst(cc):
                        sl = cc % 2
                        if cc < 16:
                            return xsr[:, sl, :], Bxsr[sl]
                        elif cc < 20:
                            return BT[:, cc - 16, :], BBT[cc - 16]
                        return CT[:, cc - 20, :], BCT[cc - 20]

                    def cD(cc):
                        sl = cc % 2
                        dst, Bd = cdst(cc)
                        Sd.op('act', I('activation', out=dst, in_=acc[:, sl, :], func=AF.Silu),
                              reads=[Bacc[sl]], writes=[Bd])

                    def cE(cc):
                        if cc >= 20:
                            return
                        dst, Bd = cdst(cc)
                        tb = 2 + (cc % 2)
                        for tt in range(4):
                            Sd.op('pe', I('transpose', out=psum[tb][:, :].bitcast(BF)[:, tt * 128:(tt + 1) * 128],
                                          in_=dst[:, tt * 128:(tt + 1) * 128], identity=identb),
                                  reads=[Bd], writes=[PB[tb]], sig=(tt == 3))

                    def cF(cc):
                        if cc >= 20:
                            return
                        tb = 2 + (cc % 2)
                        src = psum[tb][:, :].bitcast(BF)[:, 0:512].rearrange("p (t c) -> p t c", c=128)
                        if cc < 16:
                            Sd.op('dve', I('tensor_copy', out=xs_tok[:, :, cc * 128:(cc + 1) * 128], in_=src),
                                  reads=[PB[tb]], writes=[Bxs])
                        else:
                            gq = cc - 16
                            Sd.op('dve', I('tensor_copy', out=Btok[:, :, gq * 128:(gq + 1) * 128], in_=src),
                                  reads=[PB[tb]], writes=[BBtok])

                    NCC = 24 if DBG >= 2 else 0
                    for it in range(-2, NCC + 1):
                        if 0 <= it + 2 < NCC:
                            cA(it + 2)
                        if 0 <= it - 1 < NCC:
                            cE(it - 1)
                        if 0 <= it + 1 < NCC:
                            cB(it + 1)
                        if 0 <= it < NCC:
                            cD(it)
                        if 0 <= it + 1 < NCC:
                            cC(it + 1)
                        if 0 <= it - 1 < NCC:
                            cF(it - 1)
                    def prologue(tt):
                        ps_ = tt % 2
                        tsl = slice(tt * 128, (tt + 1) * 128)
                        smv = lambda k: sm[:, ps_, k, :]
                        Bs_ = Bsm[ps_]
                        for zc in range(4):
                            pb = zc % 2
                            for k in range(8):
                                Sd.op('pe', I('matmul', psum[pb][:, :], lhsT=hT[:, k, tsl],
                                              rhs=inw[:, k, zc * 512:(zc + 1) * 512], start=(k == 0), stop=(k == 7)),
                                      reads=[B_inw, BhT], writes=[PB[pb]], sig=(k == 7))
                            Sd.op('act', I('activation', out=sz[:, ps_, zc * 512:(zc + 1) * 512], in_=psum[pb][:, :],
                                           func=AF.Silu),
                                  reads=[PB[pb]], writes=[Bsz[ps_]])
                        for k in range(8):
                            Sd.op('pe', I('matmul', psum[2][:, 0:NH], lhsT=hT[:, k, tsl], rhs=inw[:, k, 5120:5152],
                                          start=(k == 0), stop=(k == 7)),
                                  reads=[B_inw, BhT], writes=[PB[2]], sig=(k == 7))
                        Sd.op('dve', I('tensor_tensor', out=smv(0), in0=psum[2][:, 0:NH], in1=dtb[:, :], op=ALU.add),
                              reads=[PB[2], B_par[2]], writes=[Bs_[0]])
                        Sd.op('act', I('activation', out=smv(1), in_=smv(0), func=AF.Exp),
                              reads=[Bs_[0]], writes=[Bs_[1]])
                        Sd.op('act', I('activation', out=smv(2), in_=smv(1), func=AF.Ln, bias=1.0),
                              reads=[Bs_[1]], writes=[Bs_[2]])
                        Sd.op('dve', I('tensor_tensor', out=smv(3), in0=smv(2), in1=aneg[:, :], op=ALU.mult),
                              reads=[Bs_[2], B_par[3]], writes=[Bs_[3]])
                        Sd.op('pe', I('matmul', psum[2][:, 64:64 + NH], lhsT=uinclf, rhs=smv(3), start=True, stop=True),
                              reads=[Bs_[3]], writes=[PB[2]], sig=False)
                        Sd.op('pe', I('matmul', psum[2][:, 128:128 + NH], lhsT=onesf, rhs=smv(3), start=True, stop=True),
                              reads=[Bs_[3]], writes=[PB[2]], sig=True)
                        acum_ps = psum[2][:, 64:64 + NH]
                        alast_ps = psum[2][:, 128:128 + NH]
                        Sd.op('dve', I('tensor_copy', out=smv(4), in_=acum_ps), reads=[PB[2]], writes=[Bs_[4]])
                        Sd.op('act', I('activation', out=smv(5), in_=acum_ps, func=AF.Exp), reads=[PB[2]], writes=[Bs_[5]])
                        Sd.op('dve', I('tensor_tensor', out=smv(6), in0=alast_ps, in1=smv(4), op=ALU.subtract),
                              reads=[PB[2], Bs_[4]], writes=[Bs_[6]])
                        Sd.op('act', I('activation', out=smv(7), in_=smv(6), func=AF.Exp), reads=[Bs_[6]], writes=[Bs_[7]])
                        Sd.op('act', I('activation', out=smv(8), in_=alast_ps, func=AF.Exp), reads=[PB[2]], writes=[Bs_[8]])
                        Sd.op('dve', I('tensor_tensor', out=smv(9), in0=smv(2), in1=smv(7), op=ALU.mult),
                              reads=[Bs_[2], Bs_[7]], writes=[Bs_[9]])
                        for gq in range(4):
                            Sd.op('pe', I('matmul', psum[3][:, gq * 128:(gq + 1) * 128], lhsT=BT[:, gq, tsl],
                                          rhs=CT[:, gq, tsl], start=True, stop=True),
                                  reads=[BBT[gq], BCT[gq]], writes=[PB[3]], sig=(gq == 3))
                        Sd.op('dve', I('tensor_tensor', out=Sm[:, ps_, :, :],
                                       in0=psum[3][:, :].rearrange("p (g t) -> p g t", t=128),
                                       in1=uincl4b.rearrange("p (g t) -> p g t", t=128), op=ALU.mult),
                              reads=[PB[3], B_cstb], writes=[BSm[ps_]])

                    def front(tt, gq):
                        ps_ = tt % 2
                        smv = lambda k, sl_: sm[:, ps_, k, sl_]
                        Bs_ = Bsm[ps_]
                        gs = gq % 2
                        gsl = slice(gq * 512, (gq + 1) * 512)
                        h8 = slice(gq * 8, (gq + 1) * 8)
                        xv = xs_tok[:, tt, gsl].rearrange("p (h d) -> p h d", d=64)
                        for (dstt, Bdst, slot) in ((xdt, Bxdt, 2), (xsD, BxsD, None), (xdtw, Bxdtw, 9)):
                            if slot is None:
                                sc, Bsc = dsk[:, h8], B_par[4]
                            else:
                                sc, Bsc = smv(slot, h8), Bs_[slot]
                            Sd.op('dve', I('tensor_tensor', out=dstt[:, gs, :].rearrange("p (h d) -> p h d", d=64),
                                           in0=xv, in1=sc.unsqueeze(2).to_broadcast([128, 8, 64]), op=ALU.mult),
                                  reads=[Bxs, Bsc], writes=[Bdst[gs]])
                        for hh in range(2):
                            hq = gq * 2 + hh
                            rs = hq % 2
                            for h4 in range(4):
                                hd = hq * 4 + h4
                                Sd.op('act', I('activation', out=R4[:, rs, h4 * 128:(h4 + 1) * 128],
                                               in_=uincl4b[:, 0:128], func=AF.Identity, scale=smv(3, slice(hd, hd + 1))),
                                      reads=[B_cstb, Bs_[3]], writes=[BR4[rs]])
                            db = 4 + (hq % 2)
                            Sd.op('pe', I('matmul', psum[db][:, :], lhsT=lstrictb, rhs=R4[:, rs, :], start=True, stop=True),
                                  reads=[BR4[rs], B_cstb], writes=[PB[db]])
                            Sd.op('act', I('activation', out=Ed[:, rs, :], in_=psum[db][:, :], func=AF.Exp),
                                  reads=[PB[db]], writes=[BEd[rs]])
                            Sd.op('dve', I('tensor_tensor', out=WT[:, gs, hh * 4:(hh + 1) * 4, :],
                                           in0=Ed[:, rs, :].rearrange("p (h t) -> p h t", t=128),
                                           in1=Sm[:, ps_, gq:gq + 1, :].to_broadcast([128, 4, 128]), op=ALU.mult),
                                  reads=[BEd[rs], BSm[ps_]], writes=[BWT[gs]])

                    def back(tt, gq):
                        ps_ = tt % 2
                        tsl = slice(tt * 128, (tt + 1) * 128)
                        smv = lambda k, sl_: sm[:, ps_, k, sl_]
                        Bs_ = Bsm[ps_]
                        gs = gq % 2
                        gsl = slice(gq * 512, (gq + 1) * 512)
                        h8 = slice(gq * 8, (gq + 1) * 8)
                        for h in range(8):
                            Sd.op('pe', I('matmul', psum[6][:, h * 64:(h + 1) * 64], lhsT=identb,
                                          rhs=xsD[:, gs, h * 64:(h + 1) * 64], start=True, stop=False),
                                  reads=[BxsD[gs], B_cstb], writes=[PB[6]], sig=False)
                            Sd.op('pe', I('matmul', psum[6][:, h * 64:(h + 1) * 64], lhsT=WT[:, gs, h, :],
                                          rhs=xdt[:, gs, h * 64:(h + 1) * 64], start=False, stop=True),
                                  reads=[BWT[gs], Bxdt[gs]], writes=[PB[6]], sig=(h == 7))
                        Sd.op('pe', I('matmul', psum[7][:, :], lhsT=CT[:, gq, tsl], rhs=stbf[:, gq, :], start=True, stop=True),
                              reads=[BCT[gq], Bstbf[gq]], writes=[PB[7]])
                        Sd.op('dve', I('tensor_tensor', out=tmp[:, gs, :].rearrange("p (h d) -> p h d", d=64),
                                       in0=psum[7][:, :].rearrange("p (h d) -> p h d", d=64),
                                       in1=smv(5, h8).unsqueeze(2).to_broadcast([128, 8, 64]), op=ALU.mult),
                              reads=[PB[7], Bs_[5]], writes=[Btmp[gs]])
                        Sd.op('dve', I('tensor_tensor', out=yg[:, gs, :], in0=psum[6][:, :], in1=tmp[:, gs, :], op=ALU.add),
                              reads=[PB[6], Btmp[gs]], writes=[Byg[gs]])
                        Sd.op('dve', I('tensor_tensor', out=yg[:, gs, :], in0=yg[:, gs, :], in1=sz[:, ps_, gsl], op=ALU.mult),
                              reads=[Byg[gs], Bsz[ps_]], writes=[Byg[gs]])
                        Sd.op('act', I('activation', out=tmp[:, gs, :], in_=yg[:, gs, :], func=AF.Square,
                                       accum_out=ys[:, gs, 0:1]),
                              reads=[Byg[gs]], writes=[Btmp[gs], Bys[gs]])
                        Sd.op('act', I('activation', out=ys[:, gs, 1:2], in_=ys[:, gs, 0:1], func=AF.Ln,
                                       scale=1.0 / 512, bias=epsc[:, 0:1]),
                              reads=[Bys[gs], B_eps], writes=[Bys[gs]])
                        Sd.op('act', I('activation', out=ys[:, gs, 2:3], in_=ys[:, gs, 1:2], func=AF.Exp, scale=-0.5),
                              reads=[Bys[gs]], writes=[Bys[gs]])
                        Sd.op('dve', I('tensor_scalar', out=ybf[:, ps_, gsl], in0=yg[:, gs, :], scalar1=ys[:, gs, 2:3],
                                       scalar2=None, op0=ALU.mult),
                              reads=[Byg[gs], Bys[gs]], writes=[Bybf[ps_]])
                        Sd.op('pe', I('matmul', psum[1][:, :], lhsT=Btok[:, tt, gq * 128:(gq + 1) * 128], rhs=xdtw[:, gs, :],
                                      start=True, stop=True),
                              reads=[BBtok, Bxdtw[gs]], writes=[PB[1]])
                        Sd.op('dve', I('tensor_tensor', out=st32[:, gq, :].rearrange("p (h d) -> p h d", d=64),
                                       in0=st32[:, gq, :].rearrange("p (h d) -> p h d", d=64),
                                       in1=smv(8, h8).unsqueeze(2).to_broadcast([128, 8, 64]), op=ALU.mult),
                              reads=[Bs_[8]], writes=[Bst32[gq]])
                        Sd.op('dve', I('tensor_tensor', out=st32[:, gq, :], in0=psum[1][:, :], in1=st32[:, gq, :], op=ALU.add),
                              reads=[PB[1]], writes=[Bst32[gq]])
                        Sd.op('act', I('activation', out=stbf[:, gq, :], in_=st32[:, gq, :], func=AF.Identity),
                              reads=[Bst32[gq]], writes=[Bstbf[gq]])

                    if DBG >= 3:
                        seq_ = [(tt, gq) for tt in range(4) for gq in range(4)]
                        prologue(0)
                        front(0, 0)
                        for idx, (tt, gq) in enumerate(seq_):
                            if gq == 1 and tt < 3:
                                prologue(tt + 1)
                            if idx + 1 < len(seq_):
                                front(*seq_[idx + 1])
                            back(tt, gq)
                            if gq == 3:
                                tok0 = grp * 512 + tt * 128
                                Sd.op('sp', I('dma_start', out=ysc[b, tok0:tok0 + 128, :], in_=ybf[:, tt % 2, :]),
                                      reads=[Bybf[tt % 2]], writes=[R_y[b][grp]], dma=Bybf[tt % 2])
                    Sd.flush()
            Sd.barrier()

    def mamba_B(li, x_src, Rsrc, x_dst, Rdst, do_kv):
        with contextlib.ExitStack() as es:
            gate_bc = cx.sb([128, nseq, D], F32, es)
            cond_gate(li, gate_bc)
            ow = cx.sb([128, 16, D], BF, es)
            B_ow = Buf("ow")
            ngT = cx.sb([128, 16], F32, es)
            B_ng = Buf("ngT")
            Sd.op('sp', I('dma_start', out=ngT[:, :], in_=m_norm_gT[li]), writes=[B_ng], dma=B_ng)
            if do_kv:
                kvw = cx.sb([128, 8, 2 * D], BF, es)
                B_kvw = Buf("kvw")
                xn = cx.sb([128, D], BF, es); Bxn = Buf("bxn")
                junk = cx.sb([128, D], BF, es); Bjunk = Buf("bjunk")
                st = cx.sb([128, 4], F32, es); Bst = Buf("bst")
                hk = cx.sb([128, 8, 512], BF, es); Bhk = Buf("hk")
                kts = cx.sb([128, 8, 512], BF, es); Bkts = Buf("kts")
                vs = cx.sb([128, 2, D], BF, es); Bvs = cx.bufs("vs", 2)
            yt = cx.sb([128, 2, DI], BF, es); Byt = cx.bufs("yt", 2)
            yT = cx.sb([128, 2, 16, 128], BF, es); ByT = cx.bufs("yT", 2)
            xt = cx.sb([128, 2, D], F32, es); Bxt = cx.bufs("bxt", 2)
            xo = cx.sb([128, 2, D], F32, es); Bxo = cx.bufs("bxo", 2)
            cast_load([(ow[:, c, :], m_out_w[li, c * 128:(c + 1) * 128, :]) for c in range(16)],
                      flat_stage(xo), Bxo, B_ow, scales=[ngT[:, c:c + 1] for c in range(16)], Bsc=B_ng)
            if do_kv:
                pcs = []
                for c in range(8):
                    for hf in range(2):
                        pcs.append((kvw[:, c, hf * 1024:(hf + 1) * 1024],
                                    kv_w[c * 128:(c + 1) * 128, hf * 1024:(hf + 1) * 1024]))
                cast_load(pcs, flat_stage(xo), Bxo, B_kvw)
            for b in range(nseq):
                for grp in range(4):
                    Sd.begin()
                    for tt in range(4):
                        t = grp * 4 + tt
                        sl = t % 2
                        tok0 = t * 128
                        Sd.op('sp', I('dma_start', out=yt[:, sl, :],
                                                                            in_=ysc[b, tok0:tok0 + 128, :]),
                              reads=[R_y[b][grp]], writes=[Byt[sl]], dma=Byt[sl])
                        Sd.op('sp', I('dma_start', out=xt[:, sl, :],
                                                                            in_=x_src[b, tok0:tok0 + 128, :]),
                              reads=[Rsrc[b][grp]], writes=[Bxt[sl]], dma=Bxt[sl])
                        for c in range(16):
                            pb = c // 4
                            Sd.op('pe', I('transpose',
                                out=psum[pb][:, :].bitcast(BF)[:, (c % 4) * 128:(c % 4 + 1) * 128],
                                in_=yt[:, sl, c * 128:(c + 1) * 128], identity=identb),
                                reads=[Byt[sl]], writes=[PB[pb]], sig=(c % 4 == 3))
                        for pb in range(4):
                            src = psum[pb][:, :].bitcast(BF)[:, 0:512].rearrange("p (c t) -> p c t", t=128)
                            if pb % 2 == 0:
                                Sd.op('act', I('activation', out=yT[:, sl, pb * 4:(pb + 1) * 4, :], in_=src,
                                               func=AF.Identity),
                                      reads=[PB[pb]], writes=[ByT[sl]])
                            else:
                                Sd.op('dve', I('tensor_copy', out=yT[:, sl, pb * 4:(pb + 1) * 4, :], in_=src),
                                      reads=[PB[pb]], writes=[ByT[sl]])
                        for hf in range(2):
                            pb = 4 + hf
                            for c in range(16):
                                Sd.op('pe', I('matmul',
                                    psum[pb][:, :], lhsT=yT[:, sl, c, :], rhs=ow[:, c, hf * 512:(hf + 1) * 512],
                                    start=(c == 0), stop=(c == 15)),
                                    reads=[ByT[sl], B_ow], writes=[PB[pb]], sig=(c == 15))
                            Sd.op('dve', I('tensor_tensor',
                                out=xo[:, sl, hf * 512:(hf + 1) * 512], in0=psum[pb][:, :],
                                in1=gate_bc[:, b, hf * 512:(hf + 1) * 512], op=ALU.mult),
                                reads=[PB[pb], B_gate], writes=[Bxo[sl]])
                        Sd.op('dve', I('tensor_tensor', out=xo[:, sl, :], in0=xo[:, sl, :],
                                                                      in1=xt[:, sl, :], op=ALU.add),
                              reads=[Bxt[sl]], writes=[Bxo[sl]])
                        Sd.op('sp', I('dma_start', out=x_dst[b, tok0:tok0 + 128, :],
                                                                            in_=xo[:, sl, :]),
                              reads=[Bxo[sl]], writes=[Rdst[b][grp]], dma=Bxo[sl])
                        if do_kv:
                            Sd.op('act', I('activation', out=junk[:, :], in_=xo[:, sl, :], func=AF.Square,
                                                                       accum_out=st[:, 0:1]),
                                  reads=[Bxo[sl]], writes=[Bjunk, Bst])
                            Sd.op('act', I('activation', out=st[:, 1:2], in_=st[:, 0:1], func=AF.Ln,
                                                                scale=1.0 / D, bias=epsc[:, 0:1]),
                                  reads=[Bst, B_eps], writes=[Bst])
                            Sd.op('act', I('activation', out=st[:, 2:3], in_=st[:, 1:2], func=AF.Exp, scale=-0.5),
                                  reads=[Bst], writes=[Bst])
                            Sd.op('dve', I('tensor_scalar', out=xn[:, :], in0=xo[:, sl, :],
                                                                          scalar1=st[:, 2:3], scalar2=None, op0=ALU.mult),
                                  reads=[Bxo[sl], Bst], writes=[Bxn])
                            for c in range(8):
                                pb = 6 + c // 4
                                Sd.op('pe', I('transpose',
                                    out=psum[pb][:, :].bitcast(BF)[:, (c % 4) * 128:(c % 4 + 1) * 128],
                                    in_=xn[:, c * 128:(c + 1) * 128], identity=identb),
                                    reads=[Bxn], writes=[PB[pb]], sig=(c % 4 == 3))
                            for c in range(8):
                                pb = 6 + c // 4
                                Sd.op('act', I('activation',
                                    out=hk[:, c, tt * 128:(tt + 1) * 128],
                                    in_=psum[pb][:, :].bitcast(BF)[:, (c % 4) * 128:(c % 4 + 1) * 128],
                                    func=AF.Identity, scale=gsT[:, 4, c, b:b + 1], bias=shT[:, 4, c, b:b + 1]),
                                    reads=[PB[pb], B_mod], writes=[Bhk])
                            for hf in range(2):
                                pb = 4 + hf
                                for k in range(8):
                                    Sd.op('pe', I('matmul',
                                        psum[pb][:, :], lhsT=hk[:, k, tt * 128:(tt + 1) * 128],
                                        rhs=kvw[:, k, D + hf * 512:D + (hf + 1) * 512], start=(k == 0), stop=(k == 7)),
                                        reads=[Bhk, B_kvw], writes=[PB[pb]], sig=(k == 7))
                                Sd.op('act', I('activation',
                                    out=vs[:, sl, hf * 512:(hf + 1) * 512], in_=psum[pb][:, :], func=AF.Identity),
                                    reads=[PB[pb]], writes=[Bvs[sl]])
                            Sd.op('sp', I('dma_start', out=vsc[b, tok0:tok0 + 128, :],
                                                                                in_=vs[:, sl, :]),
                                  reads=[Bvs[sl]], writes=[R_kv[b]], dma=Bvs[sl])
                    if do_kv:
                        for hp in range(8):
                            pb = 6 + hp % 2
                            for k in range(8):
                                Sd.op('pe', I('matmul',
                                    psum[pb][:, :], lhsT=kvw[:, k, hp * 128:(hp + 1) * 128], rhs=hk[:, k, :],
                                    start=(k == 0), stop=(k == 7)),
                                    reads=[Bhk, B_kvw], writes=[PB[pb]], sig=(k == 7))
                            Sd.op('dve', I('tensor_copy', out=kts[:, hp, :], in_=psum[pb][:, :]),
                                  reads=[PB[pb]], writes=[Bkts])
                        Sd.op('sp', I('dma_start',
                            out=ktsc[b, :, :, grp * 512:(grp + 1) * 512].rearrange("c p s -> p c s"),
                            in_=kts[:, :, :]),
                            reads=[Bkts], writes=[R_kv[b]], dma=Bkts)
                    Sd.flush()
            Sd.barrier()


    gsc = nc.dram_tensor("gsc", [4, nseq, D], F32, kind="Internal").ap()
    R_gsc = [Buf(f"gsc{i}") for i in range(4)]

    def gate_to_dram(li):
        with contextlib.ExitStack() as es:
            gate_bc = cx.sb([128, nseq, D], F32, es)
            cond_gate(li, gate_bc)
            Sd.op('sp', I('dma_start', out=gsc[li:li + 1, :, :], in_=gate_bc[0:1, :, :]),
                  reads=[B_gate], writes=[R_gsc[li]], dma=B_gate)
            Sd.barrier()

    def sb_layer(j_l, li, x_src, Rsrc, x_dst, Rdst, last):
        gate_to_dram(li)
        with contextlib.ExitStack() as es:
            inw = cx.sb([128, 8, 2 * D], BF, es); B_inw = Buf("sinw")
            ow = cx.sb([128, 8, D], BF, es); B_ow = Buf("sow")
            KT = cx.sb([128, 8, S], BF, es); B_KT = Buf("KT")
            Vt = cx.sb([128, 16, D], BF, es); B_Vt = Buf("Vt")
            nb = norm_bufs(es)
            xt, Bxt = nb[0], nb[1]
            hT, BhT = nb[8], nb[9]
            GT, BGT = hT, BhT
            QT = cx.sb([128, 8, 512], BF, es); BQT = Buf("QT")
            SZ = cx.sb([128, 8, 512], BF, es); BSZ = Buf("SZ")
            Et = [cx.sb([128, 2, 512], F32, es) for _ in range(2)]
            BE = [cx.bufs(f"E{X}", 2) for X in range(2)]
            SPt = [cx.sb([128, 3, 512], BF, es) for _ in range(2)]
            BSP = [cx.bufs(f"SP{X}", 3) for X in range(2)]
            ECt = [cx.sb([128, 2, 512], BF, es) for _ in range(2)]
            BEC = [cx.bufs(f"EC{X}", 2) for X in range(2)]
            At = [cx.sb([128, 2, 512], BF, es) for _ in range(2)]
            BA = [cx.bufs(f"A{X}", 2) for X in range(2)]
            gate1 = cx.sb([128, D], F32, es); Bg1 = Buf("gate1")
            xo = cx.sb([128, 2, D], F32, es); Bxo = cx.bufs("sxo", 2)
            pcs = []
            for c in range(8):
                for hf in range(2):
                    pcs.append((inw[:, c, hf * 1024:(hf + 1) * 1024],
                                sb_in_w[j_l, c * 128:(c + 1) * 128, hf * 1024:(hf + 1) * 1024]))
            cast_load(pcs, flat_stage(xo), Bxo, B_inw)
            cast_load([(ow[:, c, :], sb_out_w[j_l, c * 128:(c + 1) * 128, :]) for c in range(8)],
                      flat_stage(xo), Bxo, B_ow)
            if last:
                xo2 = cx.sb([128, 2, D], F32, es); Bxo2 = cx.bufs("sxo2", 2)
                fg = cx.sb([128, D], F32, es); Bfg = Buf("sfg")
                Sd.op('sp', I('dma_start', out=fg[:, :], in_=final_g.partition_broadcast(128)),
                      writes=[Bfg], dma=Bfg)
                fjunk, Bfj = nb[4], nb[5]
                fst = cx.sb([128, 2, 4], F32, es); Bfst = cx.bufs("fst2", 2)
            zbank = [[0, 1], [2, 3]]
            pbank = [4, 5]
            OB = 6
            for b in range(nseq):
                for g_ in range(4):
                    Sd.begin()
                    load_norm_transpose(nb, x_src, Rsrc, b, g_, li)
                    for hp in range(8):
                        for k in range(8):
                            Sd.op('pe', I('matmul',
                                psum[7][:, :], lhsT=inw[:, k, hp * 128:(hp + 1) * 128], rhs=hT[:, k, :],
                                start=(k == 0), stop=(k == 7)),
                                reads=[B_inw, BhT], writes=[PB[7]], sig=(k == 7))
                        Sd.op('act', I('activation', out=QT[:, hp, :], in_=psum[7][:, :],
                                                                   func=AF.Identity, scale=0.125),
                              reads=[PB[7]], writes=[BQT])
                        for k in range(8):
                            Sd.op('pe', I('matmul',
                                psum[6][:, :], lhsT=inw[:, k, D + hp * 128:D + (hp + 1) * 128], rhs=hT[:, k, :],
                                start=(k == 0), stop=(k == 7)),
                                reads=[B_inw, BhT], writes=[PB[6]], sig=(k == 7))
                        Sd.op('act', I('activation', out=SZ[:, hp, :], in_=psum[6][:, :], func=AF.Silu),
                              reads=[PB[6]], writes=[BSZ])
                    Sd.flush()
                    if g_ == 0:
                        Sd.op('sp', I('dma_start', out=KT[:, :, :], in_=ktsc[b].rearrange("c p s -> p c s")),
                              reads=[R_kv[b]], writes=[B_KT], dma=B_KT)
                        Sd.op('sp', I('dma_start', out=Vt[:, :, :], in_=vsc[b].rearrange("(k p) d -> p k d", p=128)),
                              reads=[R_kv[b]], writes=[B_Vt], dma=B_Vt)
                        Sd.op('sp', I('dma_start', out=gate1[:, :], in_=gsc[li, b:b + 1, :].partition_broadcast(128)),
                              reads=[R_gsc[li]], writes=[Bg1], dma=Bg1)
                    J = 4 * g_ + 4
                    tiles = [(hp, k) for hp in range(8) for k in range(J)]
                    NT = len(tiles)

                    def c0_of(i):
                        hp, k = tiles[i]
                        jb = J - 1 - k
                        return max(0, jb - 4 * g_) * 128

                    def s1_qk(X, i):
                        hp, k = tiles[i]
                        jb = J - 1 - k
                        c0 = c0_of(i)
                        pr = slice(X * 64, X * 64 + 64)
                        zb = zbank[X][i % 2]
                        Sd.op('pe', I('matmul', psum[zb][:, c0:512], lhsT=KT[pr, hp, jb * 128:(jb + 1) * 128],
                                      rhs=QT[pr, hp, c0:512], start=True, stop=True),
                              reads=[B_KT, BQT], writes=[PB[zb]])

                    def s1_e(X, i):
                        zb = zbank[X][i % 2]
                        c0 = c0_of(i)
                        Sd.op('act', I('activation', out=Et[X][:, i % 2, c0:512], in_=psum[zb][:, c0:512], func=AF.Exp),
                              reads=[PB[zb]], writes=[BE[X][i % 2]])

                    def s1_mask(X, i):
                        hp, k = tiles[i]
                        jb = J - 1 - k
                        if jb >= 4 * g_:
                            c0 = c0_of(i)
                            Sd.op('dve', I('tensor_tensor', out=Et[X][:, i % 2, c0:c0 + 128],
                                           in0=Et[X][:, i % 2, c0:c0 + 128],
                                           in1=cstb[:, C_MASK:C_MASK + 128], op=ALU.mult),
                                  reads=[B_cstb], writes=[BE[X][i % 2]])

                    def s1_sp(X, i):
                        c0 = c0_of(i)
                        Sd.op('act', I('activation', out=SPt[X][:, i % 3, c0:512], in_=Et[X][:, i % 2, c0:512],
                                       func=AF.Ln, bias=1.0),
                              reads=[BE[X][i % 2]], writes=[BSP[X][i % 3]])

                    def s2_cum(X, i):
                        hp, k = tiles[i]
                        pbk = pbank[X]
                        first = (k == 0)
                        c0 = c0_of(i)
                        if not first:
                            cp_ = c0_of(i - 1)
                            Sd.op('pe', I('matmul', psum[pbk][:, cp_:512], lhsT=tcb, rhs=SPt[X][:, (i - 1) % 3, cp_:512],
                                          start=False, stop=False, skip_group_check=True),
                                  reads=[BSP[X][(i - 1) % 3], B_cstb], writes=[PB[pbk]], sig=False)
                        Sd.op('pe', I('matmul', psum[pbk][:, c0:512], lhsT=tinclb, rhs=SPt[X][:, i % 3, c0:512],
                                      start=first, stop=True, skip_group_check=True),
                              reads=[BSP[X][i % 3], B_cstb], writes=[PB[pbk]])

                    def s2_ec(X, i):
                        pbk = pbank[X]
                        c0 = c0_of(i)
                        Sd.op('act', I('activation', out=ECt[X][:, i % 2, c0:512], in_=psum[pbk][:, c0:512],
                                       func=AF.Exp, scale=-1.0),
                              reads=[PB[pbk]], writes=[BEC[X][i % 2]])

                    def s2_a(X, i):
                        c0 = c0_of(i)
                        Sd.op('dve', I('tensor_tensor', out=At[X][:, i % 2, c0:512], in0=Et[X][:, i % 2, c0:512],
                                       in1=ECt[X][:, i % 2, c0:512], op=ALU.mult),
                              reads=[BE[X][i % 2], BEC[X][i % 2]], writes=[BA[X][i % 2]])

                    def s2_av(X, i):
                        hp, k = tiles[i]
                        jb = J - 1 - k
                        hd = hp * 2 + X
                        ob = 6 + (hp % 2)
                        c0 = c0_of(i)
                        Sd.op('pe', I('matmul', psum[ob][X * 64:X * 64 + 64, c0:512],
                                      lhsT=Vt[:, jb, hd * 64:(hd + 1) * 64], rhs=At[X][:, i % 2, c0:512],
                                      start=(k == 0), stop=(k == J - 1), skip_group_check=True),
                              reads=[B_Vt, BA[X][i % 2]], writes=[PB[ob]])
                        if X == 1 and k == J - 1:
                            Sd.op('dve', I('tensor_tensor', out=GT[:, hp, :], in0=psum[ob][:, :],
                                           in1=SZ[:, hp, :], op=ALU.mult),
                                  reads=[PB[ob], BSZ], writes=[BGT])

                    for it in range(-2, NT):
                        i2, i1, i0 = it + 2, it + 1, it
                        if 0 <= i1 < NT:
                            s2_cum(0, i1); s2_cum(1, i1)
                        if 0 <= i2 < NT:
                            s1_qk(0, i2); s1_qk(1, i2)
                        if 0 <= i0 < NT:
                            s2_av(0, i0); s2_av(1, i0)
                        if 0 <= i1 < NT:
                            s2_ec(0, i1); s2_ec(1, i1)
                            s2_a(0, i1); s2_a(1, i1)
                        if 0 <= i2 < NT:
                            s1_e(0, i2); s1_e(1, i2)
                            s1_mask(0, i2); s1_mask(1, i2)
                            s1_sp(0, i2); s1_sp(1, i2)
                    Sd.begin()
                    for tt in range(4):
                        t = g_ * 4 + tt
                        sl = t % 2
                        tok0 = t * 128
                        Sd.op('sp', I('dma_start', out=xt[:, sl, :],
                                                                            in_=x_src[b, tok0:tok0 + 128, :]),
                              reads=[Rsrc[b][g_]], writes=[Bxt[sl]], dma=Bxt[sl])
                        for hf in range(2):
                            pb = 0 + hf
                            for hp in range(8):
                                Sd.op('pe', I('matmul',
                                    psum[pb][:, :], lhsT=GT[:, hp, tt * 128:(tt + 1) * 128],
                                    rhs=ow[:, hp, hf * 512:(hf + 1) * 512], start=(hp == 0), stop=(hp == 7)),
                                    reads=[BGT, B_ow], writes=[PB[pb]], sig=(hp == 7))
                            Sd.op('dve', I('tensor_tensor',
                                out=xo[:, sl, hf * 512:(hf + 1) * 512], in0=psum[pb][:, :],
                                in1=gate1[:, hf * 512:(hf + 1) * 512], op=ALU.mult),
                                reads=[PB[pb], Bg1], writes=[Bxo[sl]])
                        Sd.op('dve', I('tensor_tensor', out=xo[:, sl, :], in0=xo[:, sl, :],
                                                                      in1=xt[:, sl, :], op=ALU.add),
                              reads=[Bxt[sl]], writes=[Bxo[sl]])
                        if last:
                            final_norm_store(xo[:, sl, :], Bxo[sl], xo2[:, sl, :], Bxo2[sl], fjunk, Bfj, fst, Bfst[sl],
                                             sl, fg, Bfg, b, t)
                        else:
                            Sd.op('sp', I('dma_start', out=x_dst[b, tok0:tok0 + 128, :],
                                                                                in_=xo[:, sl, :]),
                                  reads=[Bxo[sl]], writes=[Rdst[b][g_]], dma=Bxo[sl])
                    Sd.flush()
            Sd.barrier()

    cur_x, cur_R = x_in, R_xin
    if nlayers >= 1:
        mamba_A(0, x_in, R_xin)
        if DBG >= 6:
            mamba_B(0, x_in, R_xin, xa, R_xa, False)
            cur_x, cur_R = xa, R_xa
    if nlayers >= 2:
        mamba_A(1, xa, R_xa)
        mamba_B(1, xa, R_xa, xb, R_xb, True)
        cur_x, cur_R = xb, R_xb
    if nlayers >= 3:
        sb_layer(0, 2, xb, R_xb, xa, R_xa, last=(nlayers == 3))
    if nlayers >= 4:
        sb_layer(1, 3, xa, R_xa, xb, R_xb, last=True)
    if nlayers <= 2:
        final_only(cur_x, cur_R)

    Sd.barrier()
    Sd.emit()
    g.close()
    cx.es.close()
    return nc


_CACHE = {}


def prep_inputs(inputs, nseq, core):
    f = lambda a: np.ascontiguousarray(a, dtype=np.float32)
    b0 = core * nseq
    c = inputs['c'][b0:b0 + nseq]
    cT = f(c.T.reshape(8, 128, nseq).transpose(1, 0, 2))
    ada_b = inputs['ada_b']
    m = {
        'x': f(inputs['x'][b0:b0 + nseq]),
        'cT': cT,
        'ada_w': f(inputs['ada_w']),
        'ada_bT': f(ada_b[:, :2048].reshape(4, 16, 128).transpose(0, 2, 1)),
        'ada_bg': f(ada_b[:, 2048:]),
        'norm_gT': f(inputs['norm_g'].reshape(4, 8, 128).transpose(0, 2, 1)),
        'm_in_w': f(inputs['m_in_w']),
        'm_conv_wT': f(inputs['m_conv_w'].transpose(0, 2, 1).reshape(2, 24, 128, 4).transpose(0, 2, 1, 3)),
        'm_conv_bT': f(inputs['m_conv_b'].reshape(2, 24, 128).transpose(0, 2, 1)),
        'm_dt_bias': f(inputs['m_dt_bias']),
        'm_a_log': f(inputs['m_a_log']),
        'm_d': f(inputs['m_d']),
        'm_norm_gT': f(inputs['m_norm_g'].reshape(2, 16, 128).transpose(0, 2, 1)),
        'm_out_w': f(inputs['m_out_w']),
        'kv_ada_w': f(inputs['kv_ada_w']),
        'kv_ada_bT': f(inputs['kv_ada_b'].reshape(16, 128).T),
        'kv_norm_gT': f(inputs['kv_norm_g'].reshape(8, 128).T),
        'kv_w': f(inputs['kv_w']),
        'sb_in_w': f(inputs['sb_in_w']),
        'sb_out_w': f(inputs['sb_out_w']),
        'final_g': f(inputs['final_g'].reshape(1, D)),
        'cst': build_consts(),
    }
    return m


def run(inputs, nseq, ncores, nlayers=4, final=True, trace=False):
    key = (nseq, nlayers, final)
    if key not in _CACHE:
        _CACHE[key] = build_program(nseq, nlayers, final)
    nc = _CACHE[key]
    inputs = {k: np.asarray(v) for k, v in inputs.items()}
    in_maps = [prep_inputs(inputs, nseq, c) for c in range(ncores)]
    res = run_bass_kernel_spmd(nc, in_maps, core_ids=list(range(ncores)), trace=trace)
    outs = [np.asarray(r["out"]) for r in res.results]
    return np.concatenate(outs, axis=0).astype(np.float32), res


def kernel(**inputs):
    out, _ = run(inputs, 4, NCORES)
    return out
```
